# Optimizing a Trainium2 kernel written in Bass

```python
import jax, jax.numpy as jnp
from jax import lax
import numpy as np

D_MODEL = 1024
BATCH = 2
SEQ = 16384
DEPTH = 2

HEAD_DIM = 64
Q_BLOCK = 128
EPS = 1e-6
NEG_INF = -1e30
A_HEADS = 8
A_PATTERNS = ((128, 1), (512, 4), (2048, 16))
B_HEADS = 4
B_DK = 64
B_DV = 128
B_GATE_RANK = 16
B_GATE_TAU = 16.0
B_CHUNK = 64
C_HEADS = 8
C_KV_HEADS = 2
C_WINDOW = 128
D_HEADS = 8
D_Q_RANK = 384
D_KV_RANK = 256
D_NOPE = 64
D_ROPE = 32
D_V = 64
ROPE_BASE = 10000.0
D_FF = -(-8 * D_MODEL // (3 * 256)) * 256

L0_IN_SIZES = (A_HEADS * HEAD_DIM, A_HEADS * HEAD_DIM, A_HEADS * HEAD_DIM,
               B_HEADS * B_DK, B_HEADS * B_DK, B_HEADS * B_DV, B_HEADS * B_DV, B_GATE_RANK)
L0_IN = sum(L0_IN_SIZES)
L0_MIX = A_HEADS * HEAD_DIM + B_HEADS * B_DV
L1_IN_SIZES = (C_HEADS * HEAD_DIM, C_KV_HEADS * HEAD_DIM, C_KV_HEADS * HEAD_DIM,
               D_Q_RANK, D_KV_RANK, D_ROPE)
L1_IN = sum(L1_IN_SIZES)
L1_MIX = C_HEADS * HEAD_DIM + D_HEADS * D_V
N_EVEN = (DEPTH + 1) // 2
N_ODD = DEPTH // 2

kernel_name = "hybrid_dilated_gla_swasink_mla_block"


def rmsnorm(x, g):
    xf = x.astype(jnp.float32)
    y = xf * lax.rsqrt(jnp.mean(xf * xf, axis=-1, keepdims=True) + EPS)
    return (y * g.astype(jnp.float32)).astype(x.dtype)


def alibi_slopes(n):
    return jnp.asarray(np.array([2.0 ** (-8.0 * (i + 1) / n) for i in range(n)], dtype=np.float32))


def split_cols(t, sizes):
    return jnp.split(t, [int(c) for c in np.cumsum(sizes)[:-1]], axis=-1)


def with_prev_block(t, blk):
    lead = t.shape[:-2]
    L, dh = t.shape[-2:]
    tb = t.reshape(*lead, L // blk, blk, dh)
    prev = jnp.concatenate([jnp.zeros_like(tb[..., :1, :, :]), tb[..., :-1, :, :]], axis=-3)
    return jnp.concatenate([prev, tb], axis=-2)


def band_geometry(nb, blk, max_rel):
    i = jnp.arange(blk)[:, None]
    j = jnp.arange(2 * blk)[None, :]
    rel = i + blk - j
    key_pos = jnp.arange(nb)[:, None, None] * blk + j - blk
    valid = (rel >= 0) & (rel <= max_rel) & (key_pos >= 0)
    return rel.astype(jnp.float32), valid


def rope(t, pos):
    half = t.shape[-1] // 2
    freqs = ROPE_BASE ** (-jnp.arange(half, dtype=jnp.float32) / half)
    ang = pos.astype(jnp.float32)[:, None] * freqs[None, :]
    cos, sin = jnp.cos(ang)[:, None, :], jnp.sin(ang)[:, None, :]
    tf = t.astype(jnp.float32)
    t1, t2 = tf[..., :half], tf[..., half:]
    return jnp.concatenate([t1 * cos - t2 * sin, t2 * cos + t1 * sin], axis=-1).astype(t.dtype)


def dilated_branch(q, k, v, slopes, window, dilation):
    B_, H, S, dh = q.shape
    L = S // dilation
    Lp = -(-L // Q_BLOCK) * Q_BLOCK
    nb = Lp // Q_BLOCK

    def by_residue(t):
        t = t.reshape(B_, H, L, dilation, dh).transpose(0, 1, 3, 2, 4)
        return jnp.pad(t, ((0, 0), (0, 0), (0, 0), (0, Lp - L), (0, 0)))

    qb = by_residue(q).reshape(B_, H, dilation, nb, Q_BLOCK, dh)
    kb = with_prev_block(by_residue(k), Q_BLOCK)
    vb = with_prev_block(by_residue(v), Q_BLOCK).astype(jnp.float32)
    rel, valid = band_geometry(nb, Q_BLOCK, window // dilation)
    s = jnp.einsum('bhrnid,bhrnjd->bhrnij', qb, kb, preferred_element_type=jnp.float32) * dh ** -0.5
    s = s - slopes[:, None, None, None, None] * (rel * dilation)
    s = jnp.where(valid, s, NEG_INF)
    m = s.max(-1)
    p = jnp.exp(s - m[..., None])
    den = p.sum(-1)
    o = jnp.einsum('bhrnij,bhrnjd->bhrnid', p, vb) / den[..., None]

    def back(t):
        t = t.reshape(B_, H, dilation, Lp, *t.shape[5:])[:, :, :, :L]
        t = jnp.moveaxis(t, 2, 3)
        return t.reshape(B_, H, S, *t.shape[4:])

    return back(o), back(m), back(den)


def dilated_attention(q, k, v):
    slopes = alibi_slopes(A_HEADS)
    outs, maxes, dens = zip(*[dilated_branch(q, k, v, slopes, w, d) for (w, d) in A_PATTERNS])
    m_all = jnp.stack(maxes)
    wts = jnp.stack(dens) * jnp.exp(m_all - m_all.max(0))
    return jnp.einsum('pbhs,pbhsd->bhsd', wts, jnp.stack(outs)) / wts.sum(0)[..., None]


def gla(q, k, v, r, g_low, w_gate_up, b_gate, norm_g):
    B_, S, _ = q.shape
    C = B_CHUNK
    nc = S // C
    log_a = jax.nn.log_sigmoid((g_low @ w_gate_up + b_gate).astype(jnp.float32)) / B_GATE_TAU

    def chunks(t, dh):
        return t.reshape(B_, nc, C, B_HEADS, dh).transpose(0, 3, 1, 2, 4).astype(jnp.float32)

    qc = chunks(q, B_DK) * B_DK ** -0.5
    kc = chunks(k, B_DK)
    vc = chunks(v, B_DV)
    bcum = jnp.cumsum(chunks(log_a, B_DK), axis=-2)
    b_last = bcum[..., -1:, :]
    b_mid = bcum[..., C // 2 - 1:C // 2, :]
    att = jnp.einsum('bhncd,bhnsd->bhncs', qc * jnp.exp(bcum - b_mid), kc * jnp.exp(b_mid - bcum))
    att = jnp.where(jnp.tril(jnp.ones((C, C), dtype=bool)), att, 0.0)
    o_intra = jnp.einsum('bhncs,bhnse->bhnce', att, vc)
    dS = jnp.einsum('bhnsd,bhnse->bhnde', kc * jnp.exp(b_last - bcum), vc)
    decay = jnp.exp(b_last[..., 0, :])

    def step(state, inp):
        dS_n, dec_n = inp
        return dec_n[..., None] * state + dS_n, state

    init = jnp.zeros((B_, B_HEADS, B_DK, B_DV), jnp.float32)
    _, s_prev = lax.scan(step, init, (jnp.moveaxis(dS, 2, 0), jnp.moveaxis(decay, 2, 0)))
    s_prev = jnp.moveaxis(s_prev, 0, 2)
    o_inter = jnp.einsum('bhncd,bhnde->bhnce', qc * jnp.exp(bcum), s_prev)
    o = (o_intra + o_inter).transpose(0, 2, 3, 1, 4).reshape(B_, S, B_HEADS, B_DV)
    o = rmsnorm(o, norm_g)
    o = o * jax.nn.silu(r.astype(jnp.float32)).reshape(B_, S, B_HEADS, B_DV)
    return o.reshape(B_, S, B_HEADS * B_DV)


def swa_sink_attention(q, k, v, sinks):
    B_, S, _ = q.shape
    G = C_HEADS // C_KV_HEADS
    nb = S // Q_BLOCK
    qb = q.reshape(B_, nb, Q_BLOCK, C_KV_HEADS, G, HEAD_DIM).transpose(0, 3, 4, 1, 2, 5)
    kt = k.reshape(B_, S, C_KV_HEADS, HEAD_DIM).transpose(0, 2, 1, 3)
    vt = v.reshape(B_, S, C_KV_HEADS, HEAD_DIM).transpose(0, 2, 1, 3)
    kb = with_prev_block(kt, Q_BLOCK)
    vb = with_prev_block(vt, Q_BLOCK).astype(jnp.float32)
    rel, valid = band_geometry(nb, Q_BLOCK, C_WINDOW - 1)
    slopes = alibi_slopes(C_HEADS).reshape(C_KV_HEADS, G)[:, :, None, None, None]
    s = jnp.einsum('bkgnid,bknjd->bkgnij', qb, kb, preferred_element_type=jnp.float32) * HEAD_DIM ** -0.5
    s = jnp.where(valid, s - slopes * rel, NEG_INF)
    sink = sinks.astype(jnp.float32).reshape(C_KV_HEADS, G)[:, :, None, None, None]
    m = jnp.maximum(s.max(-1, keepdims=True), sink)
    p = jnp.exp(s - m)
    den = p.sum(-1, keepdims=True) + jnp.exp(sink - m)
    o = jnp.einsum('bkgnij,bknjd->bkgnid', p / den, vb)
    return o.transpose(0, 3, 4, 1, 2, 5).reshape(B_, S, C_HEADS * HEAD_DIM)


def mla(c_q, c_kv, k_rope, q_norm, w_uq, kv_norm, w_ukv):
    B_, S, _ = c_q.shape
    pos = jnp.arange(S)
    q = (rmsnorm(c_q, q_norm) @ w_uq).reshape(B_, S, D_HEADS, D_NOPE + D_ROPE)
    kv = (rmsnorm(c_kv, kv_norm) @ w_ukv).reshape(B_, S, D_HEADS, D_NOPE + D_V)
    q_nope, q_pe = q[..., :D_NOPE], rope(q[..., D_NOPE:], pos)
    k_nope, v = kv[..., :D_NOPE], kv[..., D_NOPE:].astype(jnp.float32)
    k_pe = rope(k_rope[:, :, None, :], pos)[:, :, 0, :]
    scale = (D_NOPE + D_ROPE) ** -0.5
    nb = S // Q_BLOCK
    qn_b = q_nope.reshape(B_, nb, Q_BLOCK, D_HEADS, D_NOPE).transpose(1, 0, 2, 3, 4)
    qp_b = q_pe.reshape(B_, nb, Q_BLOCK, D_HEADS, D_ROPE).transpose(1, 0, 2, 3, 4)
    kpos = jnp.arange(S)

    def block(args):
        qn, qp, n = args
        s = (jnp.einsum('bihd,bjhd->bhij', qn, k_nope, preferred_element_type=jnp.float32)
             + jnp.einsum('bihd,bjd->bhij', qp, k_pe, preferred_element_type=jnp.float32)) * scale
        qpos = n * Q_BLOCK + jnp.arange(Q_BLOCK)
        s = jnp.where(qpos[:, None] >= kpos[None, :], s, NEG_INF)
        return jnp.einsum('bhij,bjhd->bihd', jax.nn.softmax(s, axis=-1), v)

    o = lax.map(block, (qn_b, qp_b, jnp.arange(nb)))
    return o.transpose(1, 0, 2, 3, 4).reshape(B_, S, D_HEADS * D_V)


def mix_even(h, w_in, w_out, gla_w_gate_up, gla_b_gate, gla_norm):
    B_, S, _ = h.shape
    qa, ka, va, qb, kb, vb, rb, gb = split_cols(h @ w_in, L0_IN_SIZES)
    heads = lambda t: t.reshape(B_, S, A_HEADS, HEAD_DIM).transpose(0, 2, 1, 3)
    o_a = dilated_attention(heads(qa), heads(ka), heads(va))
    o_a = o_a.transpose(0, 2, 1, 3).reshape(B_, S, A_HEADS * HEAD_DIM)
    o_b = gla(qb, kb, vb, rb, gb, gla_w_gate_up, gla_b_gate, gla_norm)
    return jnp.concatenate([o_a, o_b], axis=-1).astype(h.dtype) @ w_out


def mix_odd(h, w_in, w_out, sinks, q_norm, w_uq, kv_norm, w_ukv):
    qc, kc, vc, c_q, c_kv, k_rope = split_cols(h @ w_in, L1_IN_SIZES)
    o_c = swa_sink_attention(qc, kc, vc, sinks)
    o_d = mla(c_q, c_kv, k_rope, q_norm, w_uq, kv_norm, w_ukv)
    return jnp.concatenate([o_c, o_d], axis=-1).astype(h.dtype) @ w_out


def swiglu(h, w_gate, w_up, w_down):
    return (jax.nn.silu(h @ w_gate) * (h @ w_up)) @ w_down


def setup_inputs(seed: int = 0) -> dict:
    key = jax.random.key(seed)
    ks = jax.random.split(key, 24)
    nrm = lambda k, shape, fan_in: jax.random.normal(k, shape, jnp.float32) * fan_in ** -0.5
    gain = lambda k, shape: 1.0 + 0.02 * jax.random.normal(k, shape, jnp.float32)
    return {
        "x": jax.random.normal(ks[0], (BATCH, SEQ, D_MODEL), jnp.float32),
        "norm_mix_pre": gain(ks[1], (DEPTH, D_MODEL)),
        "norm_mix_post": gain(ks[2], (DEPTH, D_MODEL)),
        "norm_ffn_pre": gain(ks[3], (DEPTH, D_MODEL)),
        "norm_ffn_post": gain(ks[4], (DEPTH, D_MODEL)),
        "ffn_w_gate": nrm(ks[5], (DEPTH, D_MODEL, D_FF), D_MODEL),
        "ffn_w_up": nrm(ks[6], (DEPTH, D_MODEL, D_FF), D_MODEL),
        "ffn_w_down": nrm(ks[7], (DEPTH, D_FF, D_MODEL), D_FF),
        "ab_w_in": nrm(ks[8], (N_EVEN, D_MODEL, L0_IN), D_MODEL),
        "ab_w_out": nrm(ks[9], (N_EVEN, L0_MIX, D_MODEL), L0_MIX),
        "gla_w_gate_up": nrm(ks[10], (N_EVEN, B_GATE_RANK, B_HEADS * B_DK), B_GATE_RANK),
        "gla_b_gate": 0.1 * jax.random.normal(ks[11], (N_EVEN, B_HEADS * B_DK), jnp.float32),
        "gla_norm": gain(ks[12], (N_EVEN, B_DV)),
        "cd_w_in": nrm(ks[13], (N_ODD, D_MODEL, L1_IN), D_MODEL),
        "cd_w_out": nrm(ks[14], (N_ODD, L1_MIX, D_MODEL), L1_MIX),
        "swa_sinks": jax.random.normal(ks[15], (N_ODD, C_HEADS), jnp.float32),
        "mla_q_norm": gain(ks[16], (N_ODD, D_Q_RANK)),
        "mla_w_uq": nrm(ks[17], (N_ODD, D_Q_RANK, D_HEADS * (D_NOPE + D_ROPE)), D_Q_RANK),
        "mla_kv_norm": gain(ks[18], (N_ODD, D_KV_RANK)),
        "mla_w_ukv": nrm(ks[19], (N_ODD, D_KV_RANK, D_HEADS * (D_NOPE + D_V)), D_KV_RANK),
    }


def reference(x, norm_mix_pre, norm_mix_post, norm_ffn_pre, norm_ffn_post,
              ffn_w_gate, ffn_w_up, ffn_w_down,
              ab_w_in, ab_w_out, gla_w_gate_up, gla_b_gate, gla_norm,
              cd_w_in, cd_w_out, swa_sinks, mla_q_norm, mla_w_uq, mla_kv_norm, mla_w_ukv):
    for layer in range(DEPTH):
        i = layer // 2
        h = rmsnorm(x, norm_mix_pre[layer])
        if layer % 2 == 0:
            y = mix_even(h, ab_w_in[i], ab_w_out[i], gla_w_gate_up[i], gla_b_gate[i], gla_norm[i])
        else:
            y = mix_odd(h, cd_w_in[i], cd_w_out[i], swa_sinks[i], mla_q_norm[i], mla_w_uq[i],
                        mla_kv_norm[i], mla_w_ukv[i])
        x = x + rmsnorm(y, norm_mix_post[layer]).astype(x.dtype)
        h = rmsnorm(x, norm_ffn_pre[layer])
        f = swiglu(h, ffn_w_gate[layer], ffn_w_up[layer], ffn_w_down[layer])
        x = x + rmsnorm(f, norm_ffn_post[layer]).astype(x.dtype)
    return x
```

```python
import contextlib
import numpy as np
import ml_dtypes
import concourse.bass as bass
import concourse.mybir as mybir
from concourse.bass_utils import run_bass_kernel_spmd

F32 = mybir.dt.float32
BF16 = mybir.dt.bfloat16
AF = mybir.ActivationFunctionType
ALU = mybir.AluOpType
NPBF = ml_dtypes.bfloat16

NCORES = 8
S = 16384
DM = 1024
TSH = 4096
TT = 512
TK = 256
DFF = 2816
EPS = 1e-6


class Buf:
    __slots__ = ("name", "lw", "rd", "rdd")

    def __init__(self, name=""):
        self.name = name
        self.lw = None
        self.rd = {}
        self.rdd = []


class Op:
    __slots__ = ("eng", "fn", "deps", "sig", "isdma", "idx", "sem", "val", "prev")

    def __init__(self, eng, fn, isdma):
        self.eng = eng
        self.fn = fn
        self.isdma = isdma
        self.deps = []
        self.sig = False
        self.sem = None
        self.val = 0
        self.prev = None


class Prog:
    EPOCH = 20000
    NDSEM = 8

    def __init__(self):
        self.nc = bass.Bass("TRN2", target_bir_lowering=False)
        self.es = contextlib.ExitStack()
        self.ops = []
        nc = self.nc
        self.eng = {"pe": nc.tensor, "act": nc.scalar, "dve": nc.vector,
                    "pool": nc.gpsimd, "sp": nc.sync}
        self.nbuf = 0

    def sb(self, name, shape, dt):
        return self.es.enter_context(self.nc.sbuf_tensor("s_" + name, list(shape), dt))

    def ps(self, name, shape=(128, 512), dt=F32):
        return self.es.enter_context(self.nc.psum_tensor("p_" + name, list(shape), dt))

    def din(self, name, shape, dt):
        return self.nc.dram_tensor(name, list(shape), dt, kind="ExternalInput").ap()

    def dout(self, name, shape, dt):
        return self.nc.dram_tensor(name, list(shape), dt, kind="ExternalOutput").ap()

    def buf(self, name=""):
        self.nbuf += 1
        return Buf(name or f"b{self.nbuf}")

    def bufs(self, n, name=""):
        return [self.buf(f"{name}{i}") for i in range(n)]

    def _record(self, o, reads, writes):
        deps = {}

        def add(d):
            if d is None or d is o:
                return
            deps[id(d)] = d

        for b in reads:
            add(b.lw)
        for b in writes:
            add(b.lw)
            for r in b.rd.values():
                add(r)
            for r in b.rdd:
                add(r)
        rawset = set(id(b.lw) for b in reads if b.lw is not None)
        best = {}
        out = []
        for d in deps.values():
            if d.isdma:
                out.append(d)
                continue
            if d.eng == o.eng and not o.isdma:
                if o.eng == "pe" or id(d) not in rawset:
                    continue
            k = d.eng
            if k not in best or best[k].idx < d.idx:
                best[k] = d
        out.extend(best.values())
        for d in out:
            d.sig = True
        o.deps = out
        for b in reads:
            if o.isdma:
                b.rdd.append(o)
            else:
                b.rd[o.eng] = o
        for b in writes:
            b.lw = o
            b.rd = {}
            b.rdd = []
        o.idx = len(self.ops)
        self.ops.append(o)
        return o

    def op(self, eng, fn, reads=(), writes=()):
        return self._record(Op(eng, fn, False), list(reads), list(writes))

    def dma(self, q, out, in_, reads=(), writes=()):
        e = self.eng[q]
        o = Op(q, (lambda: e.dma_start(out=out, in_=in_)), True)
        o.sig = True
        return self._record(o, list(reads), list(writes))

    def mm(self, out, lhsT, rhs, start, stop, reads, writes):
        nc = self.nc
        return self.op("pe", lambda: nc.tensor.matmul(out, lhsT=lhsT, rhs=rhs, start=start, stop=stop,
                                                     skip_group_check=True), reads, writes)

    def act(self, out, in_, func, reads, writes, bias=0.0, scale=1.0):
        nc = self.nc
        return self.op("act", lambda: nc.scalar.activation(out=out, in_=in_, func=func, bias=bias, scale=scale),
                       reads, writes)

    def tt(self, out, in0, in1, op, reads, writes, eng="dve"):
        e = self.eng[eng]
        return self.op(eng, lambda: e.tensor_tensor(out=out, in0=in0, in1=in1, op=op), reads, writes)

    def stt(self, out, in0, scalar, in1, op0, op1, reads, writes, eng="dve"):
        e = self.eng[eng]
        return self.op(eng, lambda: e.scalar_tensor_tensor(out=out, in0=in0, scalar=scalar, in1=in1,
                                                          op0=op0, op1=op1), reads, writes)

    def ts(self, out, in0, s1, s2, op0, op1, reads, writes, eng="dve"):
        e = self.eng[eng]
        if s2 is None:
            return self.op(eng, lambda: e.tensor_scalar(out=out, in0=in0, scalar1=s1, scalar2=None, op0=op0),
                           reads, writes)
        return self.op(eng, lambda: e.tensor_scalar(out=out, in0=in0, scalar1=s1, scalar2=s2, op0=op0, op1=op1),
                       reads, writes)

    def copy(self, out, in_, reads, writes, eng="dve"):
        if eng == "act":
            nc = self.nc
            return self.op("act", lambda: nc.scalar.copy(out=out, in_=in_), reads, writes)
        e = self.eng[eng]
        return self.op(eng, lambda: e.tensor_copy(out=out, in_=in_), reads, writes)

    def recip(self, out, in_, reads, writes):
        nc = self.nc
        return self.op("dve", lambda: nc.vector.reciprocal(out=out, in_=in_), reads, writes)

    def memset(self, ap, val, writes, eng="dve"):
        e = self.eng[eng]
        return self.op(eng, lambda: e.memset(ap, val), [], writes)

    def finalize(self):
        nc = self.nc
        es = self.es
        cnt = {}
        sems = {}
        for o in self.ops:
            if o.isdma or not o.sig:
                continue
            c = cnt.get(o.eng, 0)
            ep = c // self.EPOCH
            key = (o.eng, ep)
            if key not in sems:
                sems[key] = es.enter_context(nc.semaphore(f"s_{o.eng}_{ep}"))
            o.sem = sems[key]
            o.val = c % self.EPOCH + 1
            cnt[o.eng] = c + 1
        dsem = {}
        dcnt = {}
        dn = {}
        for o in self.ops:
            if not o.isdma:
                continue
            n = dn.get(o.eng, 0)
            k = n % self.NDSEM
            key = (o.eng, k)
            if key not in dsem:
                dsem[key] = es.enter_context(nc.semaphore(f"d_{o.eng}_{k}"))
                dcnt[key] = 0
            o.sem = dsem[key]
            o.prev = dcnt[key]
            dcnt[key] += 16
            o.val = dcnt[key]
            dn[o.eng] = n + 1
        waited = {}
        for o in self.ops:
            e = self.eng[o.eng]
            w = waited.setdefault(o.eng, {})
            need = {}
            for d in o.deps:
                k = id(d.sem)
                if k not in need or need[k][1] < d.val:
                    need[k] = (d.sem, d.val)
            if o.isdma and o.prev > 0:
                k = id(o.sem)
                if k not in need or need[k][1] < o.prev:
                    need[k] = (o.sem, o.prev)
            for k, (sm, v) in need.items():
                if w.get(k, 0) >= v:
                    continue
                e.wait_ge(sm, v)
                w[k] = v
            ins = o.fn()
            if o.isdma:
                ins.then_inc(o.sem, 16)
            elif o.sig:
                ins.then_inc(o.sem, 1)
        for key, sm in dsem.items():
            nc.sync.wait_ge(sm, dcnt[key])
        self.es.close()
        return nc


class TokCtx:
    def __init__(self, P):
        self.P = P
        self.ones = P.sb("ones", [128, 128], BF16)
        self.b_ones = P.buf("ones")
        P.memset(self.ones[:], 1.0, [self.b_ones])
        self.sq = P.sb("sq", [128, 8, TK], BF16)
        self.b_sq = P.bufs(8, "sq")
        self.sv = P.sb("sv", [128, TK], F32)
        self.b_sv = P.buf("sv")
        self.rstd = P.sb("rstd", [128, TK], F32)
        self.b_rstd = P.buf("rstd")
        self.psb = [P.ps(f"psb{i}")[:, 0:TK] for i in range(8)]
        self.b_ps = P.bufs(8, "ps")
        self.yT = P.sb("yT", [128, 8, TK], F32)
        self.b_y = P.bufs(8, "y")
        self.tmp = P.sb("tmpf", [128, 2, TK], F32)
        self.b_tmp = P.bufs(2, "tmp")
        self.ntmp = 0
        self.nps = 0

    def next_ps(self):
        i = self.nps % 6
        self.nps += 1
        return i


def load_weight(P, W, Wsb, KC, buf, rows=None):
    for kc in range(KC):
        r0 = kc * 128
        r1 = r0 + 128 if rows is None else min(r0 + 128, rows)
        P.dma("pool", Wsb[0:r1 - r0, kc, :], W[r0:r1, :], [], [buf])


def load_gain(P, g_d, KC, name, scale):
    g = P.sb(name, [128, KC], F32)
    b = P.buf(name)
    P.dma("sp", g[:], g_d[:, :], [], [b])
    P.ts(g[:], g[:], float(scale), None, ALU.mult, None, [b], [b])
    return g, b


def stats_rstd(C, KC, n, sq_bufs):
    P = C.P
    bank = 6 + (C.ntmp % 2)
    C.ntmp += 1
    ps = C.psb[bank]
    bps = C.b_ps[bank]
    for kc in range(KC):
        P.mm(ps[:, :], C.ones[:, :], C.sq[:, kc, :], kc == 0, kc == KC - 1,
             [C.b_ones, sq_bufs[kc]], [bps])
    P.act(C.sv[:], ps[:, :], AF.Ln, [bps], [C.b_sv], bias=float(n * EPS))
    P.act(C.rstd[:], C.sv[:], AF.Exp, [C.b_sv], [C.b_rstd], scale=-0.5)


def norm_fm(C, xT, b_x, KC, n, g, b_g, hT, b_h):
    P = C.P
    for kc in range(KC):
        P.act(C.sq[:, kc, :], xT[:, kc, :], AF.Square, [b_x[kc]], [C.b_sq[kc]])
    stats_rstd(C, KC, n, C.b_sq)
    for kc in range(KC):
        P.stt(hT[:, kc, :], xT[:, kc, :], g[:, kc:kc + 1], C.rstd[:], ALU.mult, ALU.mult,
              [b_x[kc], b_g, C.b_rstd], [b_h[kc]])


def linear_chunk(C, Wsb, b_w, KC, col0, M, inT, b_in, ps, bps, krows=None):
    P = C.P
    for kc in range(KC):
        kr = 128 if krows is None else krows[kc]
        P.mm(ps[0:M, :], Wsb[0:kr, kc, col0:col0 + M], inT[0:kr, kc, :], kc == 0, kc == KC - 1,
             [b_w, b_in[kc]], [bps])


def postnorm_residual(C, produce, g, b_g, xT, b_x):
    P = C.P
    for oc in range(8):
        bi = C.next_ps()
        ps, bps = C.psb[bi], C.b_ps[bi]
        produce(oc, ps, bps)
        P.act(C.sq[:, oc, :], ps[:, :], AF.Square, [bps], [C.b_sq[oc]])
        P.copy(C.yT[:, oc, :], ps[:, :], [bps, C.b_sq[oc]], [C.b_y[oc]])
    stats_rstd(C, 8, DM, C.b_sq)
    for oc in range(8):
        ti = oc % 2
        P.stt(C.tmp[:, ti, :], C.yT[:, oc, :], g[:, oc:oc + 1], C.rstd[:], ALU.mult, ALU.mult,
              [C.b_y[oc], b_g, C.b_rstd], [C.b_tmp[ti]])
        P.tt(xT[:, oc, :], xT[:, oc, :], C.tmp[:, ti, :], ALU.add, [b_x[oc], C.b_tmp[ti]], [b_x[oc]])


class FFN:
    def __init__(self, C, pfx):
        P = C.P
        self.C = C
        self.wg_d = P.din(pfx + "wg", [DM, DFF], F32)
        self.wu_d = P.din(pfx + "wu", [DM, DFF], F32)
        self.wd_d = P.din(pfx + "wd", [DFF, DM], F32)
        self.gpre_d = P.din(pfx + "gpre", [128, 8], F32)
        self.gpost_d = P.din(pfx + "gpost", [128, 8], F32)

    def alloc(self, shared):
        C, P = self.C, self.C.P
        if shared is None:
            self.wg = P.sb("wg", [128, 8, DFF], BF16)
            self.wu = P.sb("wu", [128, 8, DFF], BF16)
            self.wd = P.sb("wd", [128, 22, DM], BF16)
            self.b_wg, self.b_wu, self.b_wd = P.buf("wg"), P.buf("wu"), P.buf("wd")
            self.aT = P.sb("aT", [128, 22, TK], BF16)
            self.b_a = P.bufs(22, "a")
            self.sg = P.sb("sg", [128, 2, TK], BF16)
            self.b_sg = P.bufs(2, "sg")
            self.sgf = P.sb("sgf", [128, 2, TK], F32)
            self.b_sgf = P.bufs(2, "sgf")
        else:
            for k in ("wg", "wu", "wd", "b_wg", "b_wu", "b_wd", "aT", "b_a", "sg", "b_sg", "sgf", "b_sgf"):
                setattr(self, k, getattr(shared, k))

    def load(self, tag):
        C, P = self.C, self.C.P
        load_weight(P, self.wg_d, self.wg, 8, self.b_wg)
        load_weight(P, self.wu_d, self.wu, 8, self.b_wu)
        load_weight(P, self.wd_d, self.wd, 22, self.b_wd)
        self.gpre, self.b_gpre = load_gain(P, self.gpre_d, 8, tag + "gpre", 32.0)
        self.gpost, self.b_gpost = load_gain(P, self.gpost_d, 8, tag + "gpost", 32.0)

    def tile(self, xT, b_x, hT, b_h):
        C, P = self.C, self.C.P
        norm_fm(C, xT, b_x, 8, DM, self.gpre, self.b_gpre, hT, b_h)
        for hc in range(22):
            gi = C.next_ps()
            ui = C.next_ps()
            linear_chunk(C, self.wg, self.b_wg, 8, hc * 128, 128, hT, b_h, C.psb[gi], C.b_ps[gi])
            linear_chunk(C, self.wu, self.b_wu, 8, hc * 128, 128, hT, b_h, C.psb[ui], C.b_ps[ui])
            si = hc % 2
            P.act(self.sgf[:, si, :], C.psb[gi][:, :], AF.Tanh, [C.b_ps[gi]], [self.b_sgf[si]], scale=0.5)
            P.stt(self.sgf[:, si, :], self.sgf[:, si, :], 1.0, C.psb[gi][:, :], ALU.add, ALU.mult,
                  [self.b_sgf[si], C.b_ps[gi]], [self.b_sgf[si]])
            P.stt(self.aT[:, hc, :], self.sgf[:, si, :], 0.5, C.psb[ui][:, :], ALU.mult, ALU.mult,
                  [self.b_sgf[si], C.b_ps[ui]], [self.b_a[hc]])

        def produce(oc, ps, bps):
            for hc in range(22):
                P.mm(ps[:, :], self.wd[:, hc, oc * 128:(oc + 1) * 128], self.aT[:, hc, :],
                     hc == 0, hc == 21, [self.b_wd, self.b_a[hc]], [bps])
        postnorm_residual(C, produce, self.gpost, self.b_gpost, xT, b_x)


class MixOut:
    def __init__(self, C, pfx):
        P = C.P
        self.C = C
        self.wo_d = P.din(pfx + "wo", [DM, DM], F32)
        self.gpost_d = P.din(pfx + "gmpost", [128, 8], F32)
        self.o_d = P.din(pfx + "omix", [DM, TSH], BF16)
        self.wo = P.sb("wo", [128, 8, DM], BF16)
        self.b_wo = P.buf("wo")
        self.oT = P.sb("oT", [128, 8, TK], BF16)
        self.b_o = P.bufs(8, "o")

    def load(self, tag):
        P = self.C.P
        load_weight(P, self.wo_d, self.wo, 8, self.b_wo)
        self.gpost, self.b_gpost = load_gain(P, self.gpost_d, 8, tag + "gmpost", 32.0)

    def tile(self, t, xT, b_x):
        C, P = self.C, self.C.P
        o_v = self.o_d.rearrange("(kc p) t -> p kc t", p=128)
        P.dma("sp", self.oT[:, :, :], o_v[:, :, t * TK:(t + 1) * TK], [], self.b_o)

        def produce(oc, ps, bps):
            linear_chunk(C, self.wo, self.b_wo, 8, oc * 128, 128, self.oT, self.b_o, ps, bps)
        postnorm_residual(C, produce, self.gpost, self.b_gpost, xT, b_x)


def emit_out_chunk(C, ps, bps, M, out_d, row0, t, stage, b_stage, si):
    P = C.P
    eng = "act" if si % 2 == 0 else "dve"
    P.copy(stage[0:M, si, :], ps[0:M, :], [bps], [b_stage[si]], eng=eng)
    P.dma("sp", out_d[row0:row0 + M, t * TK:(t + 1) * TK], stage[0:M, si, :], [b_stage[si]], [])


L0_IN = 3088


def build_LA():
    global TK
    TK = 512
    try:
        return _build_LA()
    finally:
        TK = 256


def _build_LA():
    P = Prog()
    C = TokCtx(P)
    xT_d = P.din("xT", [DM, TSH], F32)
    w_d = P.din("w_in", [DM, L0_IN], F32)
    g_d = P.din("gpre", [128, 8], F32)
    out_d = P.dout("pT", [L0_IN, TSH], BF16)
    w = P.sb("w_in", [128, 8, L0_IN], BF16)
    b_w = P.buf("w")
    load_weight(P, w_d, w, 8, b_w)
    g, b_g = load_gain(P, g_d, 8, "gpre_s", 32.0)
    xT = P.sb("xTs", [128, 8, TK], F32)
    b_x = P.bufs(8, "x")
    hT = P.sb("hT", [128, 8, TK], BF16)
    b_h = P.bufs(8, "h")
    stage = P.sb("stage", [128, 4, TK], BF16)
    b_st = P.bufs(4, "st")
    x_v = xT_d.rearrange("(kc p) t -> p kc t", p=128)
    nst = 0
    for t in range(TSH // TK):
        for kc in range(8):
            P.dma("sp", xT[:, kc, :], x_v[:, kc, t * TK:(t + 1) * TK], [], [b_x[kc]])
        norm_fm(C, xT, b_x, 8, DM, g, b_g, hT, b_h)
        for oc in range(25):
            M = 128 if oc < 24 else 16
            bi = C.next_ps()
            linear_chunk(C, w, b_w, 8, oc * 128, M, hT, b_h, C.psb[bi], C.b_ps[bi])
            emit_out_chunk(C, C.psb[bi], C.b_ps[bi], M, out_d, oc * 128, t, stage, b_st, nst % 4)
            nst += 1
    return P.finalize()


L1_W = 1440 + 32
UQ_W = 512 + 256 + 256
P1_ROWS = 512 + 128 + 128 + 512 + 256 + 1024 + 32


def build_LCE(with_l1):
    global TK
    TK = 512 if with_l1 else 256
    try:
        return _build_LCE(with_l1)
    finally:
        TK = 256


def _build_LCE(with_l1):
    P = Prog()
    C = TokCtx(P)
    xT_d = P.din("xT", [DM, TSH], F32)
    xo_d = None if with_l1 else P.dout("xoT", [DM, TSH], F32)
    if not with_l1:
        mo = MixOut(C, "m_")
        ff = FFN(C, "f_")
        ff.alloc(None)
        mo.load("m")
        ff.load("f")
    xT = P.sb("xTs", [128, 8, TK], F32)
    b_x = P.bufs(8, "x")
    hT = P.sb("hT", [128, 8, TK], BF16)
    b_h = P.bufs(8, "h")
    x_v = xT_d.rearrange("(kc p) t -> p kc t", p=128)
    xo_v = None if with_l1 else xo_d.rearrange("(kc p) t -> p kc t", p=128)
    if with_l1:
        w1_d = P.din("w_in1", [DM, L1_W], F32)
        g1_d = P.din("gpre1", [128, 8], F32)
        wuq_d = P.din("w_uq", [384, UQ_W], F32)
        wukv_d = P.din("w_ukv", [256, 1024], F32)
        gq_d = P.din("gq", [128, 3], F32)
        gkv_d = P.din("gkv", [128, 2], F32)
        cos_d = P.din("cosT", [128, TSH], F32)
        sin_d = P.din("sinT", [128, TSH], F32)
        p1_d = P.dout("p1T", [P1_ROWS, TSH], BF16)
        w1 = P.sb("w_in1", [128, 8, L1_W], BF16)
        b_w1 = P.buf("w1")
        wuq = P.sb("w_uq", [128, 3, UQ_W], BF16)
        b_wuq = P.buf("wuq")
        wukv = P.sb("w_ukv", [128, 2, 1024], BF16)
        b_wukv = P.buf("wukv")
        load_weight(P, w1_d, w1, 8, b_w1)
        load_weight(P, wuq_d, wuq, 3, b_wuq)
        load_weight(P, wukv_d, wukv, 2, b_wukv)
        g1, b_g1 = load_gain(P, g1_d, 8, "g1s", 32.0)
        gq, b_gq = load_gain(P, gq_d, 3, "gqs", float(np.sqrt(384.0)))
        gkv, b_gkv = load_gain(P, gkv_d, 2, "gkvs", 16.0)
        stage = P.sb("stage", [128, 4, TK], BF16)
        b_st = P.bufs(4, "st")
        cT = P.sb("cT", [128, 5, TK], F32)
        b_c = P.bufs(5, "c")
        cn = P.sb("cn", [128, 5, TK], BF16)
        b_cn = P.bufs(5, "cn")
        cs = P.sb("cs", [128, TK], F32)
        sn = P.sb("sn", [128, TK], F32)
        b_cs, b_sn = P.buf("cs"), P.buf("sn")
        rt = P.sb("rt", [128, 2, TK], F32)
        b_rt = P.bufs(2, "rt")
    nst = 0
    b_xo = P.bufs(TSH // TK, "xo")
    for t in range(TSH // TK if _NTILES[0] is None else _NTILES[0]):
        for kc in range(8):
            P.dma("sp", xT[:, kc, :], x_v[:, kc, t * TK:(t + 1) * TK], [], [b_x[kc]])
        if not with_l1:
            mo.tile(t, xT, b_x)
            for kc in range(8):
                P.dma("sp", xo_v[:, kc, t * TK:(t + 1) * TK], xT[:, kc, :], [b_x[kc]], [b_xo[t]])
            continue
        norm_fm(C, xT, b_x, 8, DM, g1, b_g1, hT, b_h)
        P.dma("sp", cs[:], cos_d[:, t * TK:(t + 1) * TK], [], [b_cs])
        P.dma("sp", sn[:], sin_d[:, t * TK:(t + 1) * TK], [], [b_sn])
        for oc in range(6):
            bi = C.next_ps()
            linear_chunk(C, w1, b_w1, 8, oc * 128, 128, hT, b_h, C.psb[bi], C.b_ps[bi])
            emit_out_chunk(C, C.psb[bi], C.b_ps[bi], 128, p1_d, oc * 128, t, stage, b_st, nst % 4)
            nst += 1
        for i in range(5):
            bi = C.next_ps()
            linear_chunk(C, w1, b_w1, 8, 768 + i * 128, 128, hT, b_h, C.psb[bi], C.b_ps[bi])
            P.copy(cT[:, i, :], C.psb[bi][:, :], [C.b_ps[bi]], [b_c[i]])
        bi, bj = C.next_ps(), C.next_ps()
        linear_chunk(C, w1, b_w1, 8, 1408, 32, hT, b_h, C.psb[bi], C.b_ps[bi])
        linear_chunk(C, w1, b_w1, 8, 1440, 32, hT, b_h, C.psb[bj], C.b_ps[bj])
        P.tt(rt[0:32, 0, :], C.psb[bi][0:32, :], cs[0:32, :], ALU.mult, [C.b_ps[bi], b_cs], [b_rt[0]])
        P.tt(rt[0:32, 1, :], C.psb[bj][0:32, :], sn[0:32, :], ALU.mult, [C.b_ps[bj], b_sn], [b_rt[1]])
        si = nst % 4
        nst += 1
        P.tt(stage[0:32, si, :], rt[0:32, 0, :], rt[0:32, 1, :], ALU.add, [b_rt[0], b_rt[1]], [b_st[si]])
        P.dma("sp", p1_d[2560:2592, t * TK:(t + 1) * TK], stage[0:32, si, :], [b_st[si]], [])
        for (c0, kcn, n, g, b_g) in ((0, 3, 384, gq, b_gq), (3, 2, 256, gkv, b_gkv)):
            for kc in range(kcn):
                P.act(C.sq[:, kc, :], cT[:, c0 + kc, :], AF.Square, [b_c[c0 + kc]], [C.b_sq[kc]])
            stats_rstd(C, kcn, n, C.b_sq)
            for kc in range(kcn):
                P.stt(cn[:, c0 + kc, :], cT[:, c0 + kc, :], g[:, kc:kc + 1], C.rstd[:], ALU.mult, ALU.mult,
                      [b_c[c0 + kc], b_g, C.b_rstd], [b_cn[c0 + kc]])
        for oc in range(4):
            bi = C.next_ps()
            linear_chunk(C, wuq, b_wuq, 3, oc * 128, 128, cn, b_cn[0:3], C.psb[bi], C.b_ps[bi])
            emit_out_chunk(C, C.psb[bi], C.b_ps[bi], 128, p1_d, 768 + oc * 128, t, stage, b_st, nst % 4)
            nst += 1
        for oc in range(2):
            bi, bj = C.next_ps(), C.next_ps()
            linear_chunk(C, wuq, b_wuq, 3, 512 + oc * 128, 128, cn, b_cn[0:3], C.psb[bi], C.b_ps[bi])
            linear_chunk(C, wuq, b_wuq, 3, 768 + oc * 128, 128, cn, b_cn[0:3], C.psb[bj], C.b_ps[bj])
            P.tt(rt[:, 0, :], C.psb[bi][:, :], cs[:], ALU.mult, [C.b_ps[bi], b_cs], [b_rt[0]])
            P.tt(rt[:, 1, :], C.psb[bj][:, :], sn[:], ALU.mult, [C.b_ps[bj], b_sn], [b_rt[1]])
            si = nst % 4
            nst += 1
            P.tt(stage[:, si, :], rt[:, 0, :], rt[:, 1, :], ALU.add, [b_rt[0], b_rt[1]], [b_st[si]])
            P.dma("sp", p1_d[1280 + oc * 128:1408 + oc * 128, t * TK:(t + 1) * TK], stage[:, si, :],
                  [b_st[si]], [])
        cnkv = cn[:, 3:5, :]
        for oc in range(8):
            bi = C.next_ps()
            P2 = C.P
            for kc in range(2):
                P2.mm(C.psb[bi][:, :], wukv[:, kc, oc * 128:(oc + 1) * 128], cn[:, 3 + kc, :],
                      kc == 0, kc == 1, [b_wukv, b_cn[3 + kc]], [C.b_ps[bi]])
            emit_out_chunk(C, C.psb[bi], C.b_ps[bi], 128, p1_d, 1536 + oc * 128, t, stage, b_st, nst % 4)
            nst += 1
    if not with_l1:
        for t in range(TSH // TK if _NTILES[0] is None else _NTILES[0]):
            for kc in range(8):
                P.dma("sp", xT[:, kc, :], xo_v[:, kc, t * TK:(t + 1) * TK], [b_xo[t]], [b_x[kc]])
            ff.tile(xT, b_x, hT, b_h)
            for kc in range(8):
                P.dma("sp", xo_v[:, kc, t * TK:(t + 1) * TK], xT[:, kc, :], [b_x[kc]], [b_xo[t]])
    return P.finalize()


NB = S // 128
NQT = S // TT
_MAXT = [NQT]


class AttnCtx:
    def __init__(self, P, nmask, NS=4, LOOK=2):
        self.P = P
        self.KT = P.sb("KT", [128, S], BF16)
        self.QT = P.sb("QT", [128, S], BF16)
        self.VA = P.sb("VA", [128, NB, 128], BF16)
        self.b_KT, self.b_QT, self.b_VA = P.buf("KT"), P.buf("QT"), P.buf("VA")
        self.masks = P.sb("masks", [128, nmask, TT], BF16)
        self.b_masks = P.buf("masks")
        self.kaug = P.sb("kaug", [3, 128], BF16)
        self.qaug = P.sb("qaug", [3, 2, TT], BF16)
        self.b_kaug, self.b_qaug = P.buf("kaug"), P.buf("qaug")
        self.btab = P.sb("btab", [128, 2, 20], F32)
        self.b_btab = P.buf("btab")
        self.NS = NS
        self.LOOK = LOOK
        self.ps_s = [P.ps(f"ps_s{i}") for i in range(self.NS)]
        self.b_ps_s = P.bufs(self.NS, "ps_s")
        self.ps_acc = [P.ps(f"ps_acc{i}") for i in range(2)]
        self.b_ps_acc = P.bufs(2, "ps_acc")
        self.pe_ = P.sb("pexp", [128, self.NS, TT], BF16)
        self.b_pe = P.bufs(self.NS, "pexp")
        self.pm = P.sb("pm", [128, self.NS, TT], BF16)
        self.b_pm = P.bufs(self.NS, "pm")
        self.rden = P.sb("rden", [128, 2, TT], F32)
        self.b_rden = P.bufs(2, "rden")
        self.ost = P.sb("ost", [64, 2, TT], BF16)
        self.b_ost = P.bufs(2, "ost")
        self.ntile = 0
        self.nblk = 0


def attention_head(A, krow0, K, scale, blocks_of_tile, aug_h, out_d, orow0, den_add=None, btab_h=None):
    P = A.P
    work = []
    for t in range(min(NQT, _MAXT[0])):
        bl = blocks_of_tile(t)
        for i, b in enumerate(bl):
            work.append((t, i == 0, i == len(bl) - 1) + tuple(b))
    LOOK = A.LOOK

    def emit_S(w):
        t, first, last, kb, qlo, qhi, mi, bias = w
        si = A.nblk % A.NS
        A.nblk += 1
        ps, bps = A.ps_s[si], A.b_ps_s[si]
        q0 = t * TT
        P.mm(ps[:, qlo:qhi], A.KT[krow0:krow0 + K, kb * 128:(kb + 1) * 128],
             A.QT[krow0:krow0 + K, q0 + qlo:q0 + qhi], True, aug_h is None, [A.b_KT, A.b_QT], [bps])
        if aug_h is not None:
            P.mm(ps[:, qlo:qhi], A.kaug[0:3, :], A.qaug[0:3, aug_h, qlo:qhi], False, True,
                 [A.b_kaug, A.b_qaug], [bps])
        if btab_h is not None:
            bias_ap = A.btab[:, btab_h, bias:bias + 1]
            P.op("act", lambda: P.nc.scalar.activation(out=A.pe_[:, si, qlo:qhi], in_=ps[:, qlo:qhi], func=AF.Exp,
                                                       bias=bias_ap, scale=float(scale)),
                 [bps, A.b_btab], [A.b_pe[si]])
        else:
            P.act(A.pe_[:, si, qlo:qhi], ps[:, qlo:qhi], AF.Exp, [bps], [A.b_pe[si]], bias=0.0,
                  scale=float(scale))
        if mi is not None:
            P.tt(A.pm[:, si, qlo:qhi], A.pe_[:, si, qlo:qhi], A.masks[:, mi, qlo:qhi], ALU.mult,
                 [A.b_pe[si], A.b_masks], [A.b_pm[si]])
            return (A.pm, A.b_pm[si], si)
        return (A.pe_, A.b_pe[si], si)

    pend = []
    for wi in range(len(work) + LOOK):
        if wi < len(work):
            pend.append((work[wi], emit_S(work[wi])))
        if wi >= LOOK:
            w, (pt, bpt, si) = pend.pop(0)
            t, first, last, kb, qlo, qhi, mi, bias = w
            ai = (A.ntile + t) % 2
            acc, bacc = A.ps_acc[ai], A.b_ps_acc[ai]
            P.mm(acc[:, qlo:qhi], A.VA[:, kb, :], pt[:, si, qlo:qhi], first, last, [A.b_VA, bpt], [bacc])
            if last:
                if den_add is not None:
                    dt_, db_, dc_ = den_add
                    P.ts(A.rden[64:128, ai, :], acc[64:128, :], dt_[64:128, dc_:dc_ + 1], None, ALU.add, None,
                         [bacc, db_], [A.b_rden[ai]])
                    P.recip(A.rden[64:128, ai, :], A.rden[64:128, ai, :], [A.b_rden[ai]], [A.b_rden[ai]])
                else:
                    P.recip(A.rden[64:128, ai, :], acc[64:128, :], [bacc], [A.b_rden[ai]])
                P.tt(A.ost[:, ai, :], acc[0:64, :], A.rden[64:128, ai, :], ALU.mult,
                     [bacc, A.b_rden[ai]], [A.b_ost[ai]])
                P.dma("sp", out_d[orow0:orow0 + 64, t * TT:(t + 1) * TT], A.ost[:, ai, :], [A.b_ost[ai]], [])
    A.ntile += NQT


def load_vaug(P, A, v_d):
    for i in range(4):
        P.dma("sp", A.VA[:, i * 32:(i + 1) * 32, :], v_d[:, i * 32:(i + 1) * 32, :], [], [A.b_VA])


def mult_A(rel):
    ok = lambda c: c.astype(np.float32)
    m = ok((rel >= 0) & (rel <= 128)) + ok((rel >= 0) & (rel <= 512) & (rel % 4 == 0)) \
        + ok((rel >= 0) & (rel <= 2048) & (rel % 16 == 0))
    return m


def make_masks(ms, fn):
    jk = np.arange(128)[:, None]
    iq = np.arange(TT)[None, :]
    out = np.zeros((128, len(ms), TT), np.float32)
    for i, m in enumerate(ms):
        out[:, i, :] = fn(128 * m + iq - jk)
    return out.astype(NPBF)


def make_aug(slopes):
    jk = np.arange(128, dtype=np.float32)
    iq = np.arange(TT)
    kaug = np.stack([np.ones(128, np.float32), np.ones(128, np.float32), jk]).astype(NPBF)
    qa = np.zeros((3, len(slopes), TT), np.float32)
    for i, s in enumerate(slopes):
        lo = (iq % 256).astype(np.float32)
        hi = (iq - iq % 256).astype(np.float32)
        qa[0, i] = -8.0 * s * lo
        qa[1, i] = -8.0 * s * hi
        qa[2, i] = 8.0 * s
    return kaug, qa.astype(NPBF)


MS_A = list(range(-3, 17))
MS_C = list(range(-3, 2))
MS_D = list(range(-3, 1))


def trim(m):
    qlo = max(0, -128 * m)
    return qlo


def build_LB(slopes_by_core_is_data=True):
    P = Prog()
    A = AttnCtx(P, len(MS_A), NS=5, LOOK=3)
    qa_d = [P.din(f"qaT{i}", [67, S], BF16) for i in range(2)]
    ka_d = [P.din(f"kaT{i}", [67, S], BF16) for i in range(2)]
    v_d = [P.din(f"v{i}", [128, NB, 128], BF16) for i in range(2)]
    masks_d = P.din("masks", [128, len(MS_A), TT], BF16)
    oa_d = P.dout("oaT", [128, S], BF16)
    P.dma("sp", A.masks[:], masks_d[:, :, :], [], [A.b_masks])
    btab_d = P.din("btab", [128, 2, 20], F32)
    P.dma("sp", A.btab[:], btab_d[:, :, :], [], [A.b_btab])
    return P, A, (qa_d, ka_d, v_d), oa_d


def load_qk(P, A, q_d, k_d, rows):
    for i in range(4):
        cs_ = slice(i * 4096, (i + 1) * 4096)
        if k_d is not None:
            P.dma("sp", A.KT[0:rows, cs_], k_d[:, cs_], [], [A.b_KT])
        if q_d is not None:
            P.dma("sp", A.QT[0:rows, cs_], q_d[:, cs_], [], [A.b_QT])


def blocks_A(slope):
    def f(t):
        out = []
        for kb in range(max(0, 4 * t - 16), 4 * t + 4):
            m = 4 * t - kb
            out.append((kb, trim(m), TT, m + 3, m + 3))
        return out
    return f


def blocks_C(slope):
    def f(t):
        out = []
        for kb in range(max(0, 4 * t - 1), 4 * t + 4):
            m = 4 * t - kb
            qlo = trim(m)
            qhi = min(TT, 128 * (2 - m)) if m <= 1 else TT
            out.append((kb, qlo, qhi, m + 3, m + 3))
        return out
    return f


def blocks_D(t):
    out = []
    for kb in range(0, 4 * t + 4):
        m = 4 * t - kb
        out.append((kb, trim(m), TT, (5 + m + 3) if m <= 0 else None, 0))
    return out


def gla_head(P, A):
    nc = P.nc
    q_d = P.din("gq", [64, S], BF16)
    k_d = P.din("gk", [64, S], BF16)
    r_d = P.din("gr", [128, S], BF16)
    g_d = P.din("gg", [16, S], BF16)
    v_d = P.din("gv", [128, NB, 128], BF16)
    wup_d = P.din("gwup", [16, 64], F32)
    nb_d = P.din("gnb", [64, 1], F32)
    gn_d = P.din("ggn", [128, 1], F32)
    tri_d = P.din("gtri", [128, 128], BF16)
    ob_d = P.dout("obT", [128, S], BF16)

    wup = P.sb("gwup", [16, 64], BF16)
    b_wup = P.buf()
    P.dma("pool", wup[:], wup_d[:, :], [], [b_wup])
    negb = P.sb("gnegb", [64, 1], F32)
    b_negb = P.buf()
    P.dma("sp", negb[:], nb_d[:, :], [], [b_negb])
    P.ts(negb[:], negb[:], -1.0, None, ALU.mult, None, [b_negb], [b_negb])
    gn = P.sb("ggn", [128, 1], F32)
    b_gn = P.buf()
    P.dma("sp", gn[:], gn_d[:, :], [], [b_gn])
    P.ts(gn[:], gn[:], float(0.5 * np.sqrt(128.0)), None, ALU.mult, None, [b_gn], [b_gn])
    tri = P.sb("gtri", [128, 128], BF16)
    b_tri = P.buf()
    P.dma("sp", tri[:], tri_d[:, :], [], [b_tri])
    ident = P.sb("gident", [64, 64], BF16)
    b_ident = P.buf()
    ident_d = P.din("gident", [64, 64], BF16)
    P.dma("sp", ident[:], ident_d[:, :], [], [b_ident])
    ones = P.sb("gones", [128, 128], BF16)
    b_ones = P.buf()
    P.memset(ones[:], 1.0, [b_ones])

    qk = P.sb("gqk", [64, 2, TT], BF16)
    b_qk = P.buf()
    g_t = P.sb("gg_t", [16, TT], BF16)
    b_g = P.buf()
    e_t = P.sb("ge", [64, TT], F32)
    b_e = P.buf()
    cab = [P.sb(f"gc{i}", [64, 8, 96], F32) for i in range(2)]
    b_ca = P.bufs(2, "gc")
    for i in range(2):
        P.memset(cab[i][:, :, 0:32], 0.0, [b_ca[i]])
    d1 = P.sb("gd1", [64, 8, 64], F32)
    d4 = P.sb("gd4", [64, 8, 64], F32)
    b_d1, b_d4 = P.buf(), P.buf()
    E = P.sb("gE", [64, 4, TT], F32)
    b_E = P.bufs(4, "gE")
    r_t2 = [P.sb(f"gr_t{i}", [128, TT], BF16) for i in range(2)]
    b_r2 = P.bufs(2, "gr")
    v_t2 = [P.sb(f"gv_t{i}", [128, 4, 128], BF16) for i in range(2)]
    b_v2 = P.bufs(2, "gv")
    dec2 = [P.sb(f"gdec{i}", [64, 8], F32) for i in range(2)]
    b_dec2 = P.bufs(2, "gdec")
    QK2 = [P.sb(f"gQK{i}", [64, 4, TT], BF16) for i in range(2)]
    b_QK2 = [P.bufs(4, f"gQK{i}_") for i in range(2)]
    K42 = [P.sb(f"gK4{i}", [128, 4, 64], BF16) for i in range(2)]
    b_K42 = P.bufs(2, "gK4")
    attm = P.sb("gattm", [128, 2, 128], BF16)
    b_attm = P.bufs(2, "gattm")
    St = P.sb("gS", [64, 128], F32)
    Sb = P.sb("gSb", [64, 128], BF16)
    b_S, b_Sb = P.buf(), P.buf()
    P.memset(St[:], 0.0, [b_S])
    P.memset(Sb[:], 0.0, [b_Sb])
    sq = P.sb("gsq", [128, TT], BF16)
    oT = P.sb("goT", [128, TT], F32)
    sv = P.sb("gsv", [128, TT], F32)
    rs = P.sb("grs", [128, TT], F32)
    sr = P.sb("gsr", [128, TT], F32)
    ot = P.sb("got", [128, TT], BF16)
    b_sq, b_oT, b_sv, b_rs, b_sr, b_ot = (P.buf() for _ in range(6))
    tr_ps = P.ps("gtr_ps", [128, 4, 64], BF16)
    b_tr = P.buf()
    ss_ps, b_ss = A.ps_s[4], A.b_ps_s[4]
    z_ps, b_z = A.ps_s[0], A.b_ps_s[0]
    o_ps, b_o = A.ps_s[1], A.b_ps_s[1]
    att_ps, b_att = A.ps_s[2], A.b_ps_s[2]
    dS_ps, b_dS = A.ps_s[3], A.b_ps_s[3]
    v_v = v_d
    NT = min(NQT, _MAXT[0])

    def setup(t):
        par = t % 2
        c0 = t * TT
        r_t, b_r, v_t, b_v = r_t2[par], b_r2[par], v_t2[par], b_v2[par]
        dec, b_dec, QK, b_QK, K4, b_K4 = dec2[par], b_dec2[par], QK2[par], b_QK2[par], K42[par], b_K42[par]
        P.dma("sp", qk[:, 0, :], q_d[:, c0:c0 + TT], [], [b_qk])
        P.dma("sp", qk[:, 1, :], k_d[:, c0:c0 + TT], [], [b_qk])
        P.dma("sp", r_t[:], r_d[:, c0:c0 + TT], [], [b_r])
        P.dma("sp", g_t[:], g_d[:, c0:c0 + TT], [], [b_g])
        P.dma("sp", v_t[:], v_v[:, t * 4:(t + 1) * 4, :], [], [b_v])
        P.mm(z_ps[0:64, :], wup[0:16, :], g_t[0:16, :], True, True, [b_wup, b_g], [b_z])
        P.op("act", lambda: nc.scalar.activation(out=e_t[:], in_=z_ps[0:64, :], func=AF.Exp,
                                                 bias=negb[:, 0:1], scale=-1.0), [b_z, b_negb], [b_e])
        P.act(cab[0][:, :, 32:96], e_t[:].rearrange("p (c s) -> p c s", s=64), AF.Ln, [b_e], [b_ca[0]], bias=1.0)
        cur = 0
        for s_ in (1, 2, 4, 8, 16, 32):
            src, dst = cab[cur], cab[1 - cur]
            P.tt(dst[:, :, 32:96], src[:, :, 32:96], src[:, :, 32 - s_:96 - s_], ALU.add,
                 [b_ca[cur]], [b_ca[1 - cur]])
            cur = 1 - cur
        bcs, b_bcs = cab[cur], b_ca[cur]
        P.tt(d1[:], bcs[:, :, 32:96], bcs[:, :, 63:64].to_broadcast([64, 8, 64]), ALU.subtract, [b_bcs], [b_d1])
        P.tt(d4[:], bcs[:, :, 32:96], bcs[:, :, 95:96].to_broadcast([64, 8, 64]), ALU.subtract, [b_bcs], [b_d4])
        d1f = d1[:].rearrange("p c s -> p (c s)")
        d4f = d4[:].rearrange("p c s -> p (c s)")
        P.act(E[:, 0, :], d1f, AF.Exp, [b_d1], [b_E[0]], scale=-1.0 / 16)
        P.act(E[:, 1, :], d1f, AF.Exp, [b_d1], [b_E[1]], scale=1.0 / 16)
        P.act(E[:, 2, :].rearrange("p (c s) -> p c s", s=64), bcs[:, :, 32:96], AF.Exp, [b_bcs], [b_E[2]],
              scale=-1.0 / 16)
        P.act(E[:, 3, :], d4f, AF.Exp, [b_d4], [b_E[3]], scale=1.0 / 16)
        P.act(dec[:, :], bcs[:, :, 95], AF.Exp, [b_bcs], [b_dec], scale=-1.0 / 16)
        P.stt(QK[:, 0, :], qk[:, 0, :], 0.125, E[:, 0, :], ALU.mult, ALU.mult, [b_qk, b_E[0]], [b_QK[0]])
        P.tt(QK[:, 1, :], qk[:, 1, :], E[:, 1, :], ALU.mult, [b_qk, b_E[1]], [b_QK[1]])
        P.stt(QK[:, 2, :], qk[:, 0, :], 0.125, E[:, 2, :], ALU.mult, ALU.mult, [b_qk, b_E[2]], [b_QK[2]])
        P.tt(QK[:, 3, :], qk[:, 1, :], E[:, 3, :], ALU.mult, [b_qk, b_E[3]], [b_QK[3]])
        for blk in range(4):
            P.op("pe", lambda blk=blk, QK=QK: nc.tensor.transpose(tr_ps[:, blk, :],
                                                                  QK[:, 3, blk * 128:(blk + 1) * 128],
                                                                  ident[:, :]), [b_QK[3], b_ident], [b_tr])
        P.copy(K4[:], tr_ps[:], [b_tr], [b_K4], eng="act")

    def chunks(t):
        par = t % 2
        v_t, b_v = v_t2[par], b_v2[par]
        dec, b_dec, QK, b_QK, K4, b_K4 = dec2[par], b_dec2[par], QK2[par], b_QK2[par], K42[par], b_K42[par]
        for blk in range(4):
            bs = slice(blk * 128, (blk + 1) * 128)
            ai = blk % 2
            P.mm(att_ps[:, bs], QK[:, 1, bs], QK[:, 0, bs], True, True, [b_QK[1], b_QK[0]], [b_att])
            P.tt(attm[:, ai, :], att_ps[:, bs], tri[:], ALU.mult, [b_att, b_tri], [b_attm[ai]])
            P.mm(o_ps[:, bs], v_t[:, blk, :], attm[:, ai, :], True, False, [b_v, b_attm[ai]], [b_o])
            for ch in range(2):
                c = blk * 2 + ch
                cs_ = slice(c * 64, (c + 1) * 64)
                P.mm(o_ps[:, cs_], Sb[:, :], QK[:, 2, cs_], False, ch == 1, [b_Sb, b_QK[2]], [b_o])
                dcol = slice((c % 4) * 128, (c % 4 + 1) * 128)
                P.mm(dS_ps[0:64, dcol], K4[ch * 64:(ch + 1) * 64, blk, :], v_t[ch * 64:(ch + 1) * 64, blk, :],
                     True, True, [b_K4, b_v], [b_dS])
                P.stt(St[:], St[:], dec[:, c:c + 1], dS_ps[0:64, dcol], ALU.mult, ALU.add,
                      [b_S, b_dec, b_dS], [b_S])
                P.copy(Sb[:], St[:], [b_S], [b_Sb], eng="act")

    def normout(t):
        par = t % 2
        c0 = t * TT
        r_t, b_r = r_t2[par], b_r2[par]
        P.act(sq[:], o_ps[:, :], AF.Square, [b_o], [b_sq])
        P.copy(oT[:], o_ps[:, :], [b_o, b_sq], [b_oT])
        P.mm(ss_ps[:, :], ones[:, :], sq[:], True, True, [b_ones, b_sq], [b_ss])
        P.act(sv[:], ss_ps[:, :], AF.Ln, [b_ss], [b_sv], bias=float(128 * EPS))
        P.act(rs[:], sv[:], AF.Exp, [b_sv], [b_rs], scale=-0.5)
        P.act(sr[:], r_t[:], AF.Tanh, [b_r], [b_sr], scale=0.5)
        P.stt(sr[:], sr[:], 1.0, r_t[:], ALU.add, ALU.mult, [b_sr, b_r], [b_sr])
        P.stt(oT[:], oT[:], gn[:, 0:1], rs[:], ALU.mult, ALU.mult, [b_oT, b_gn, b_rs], [b_oT])
        P.tt(ot[:], oT[:], sr[:], ALU.mult, [b_oT, b_sr], [b_ot])
        P.dma("sp", ob_d[:, c0:c0 + TT], ot[:], [b_ot], [])

    if NT > 0:
        setup(0)
    for t in range(NT):
        if t + 1 < NT:
            setup(t + 1)
        chunks(t)
        normout(t)


_LBPARTS = {"attn": True, "gla": True}
_NHEAD = [2]


def build_LB_prog():
    P, A, (qa_d, ka_d, v_d), oa_d = build_LB()
    if _LBPARTS["attn"]:
        for hi in range(_NHEAD[0]):
            load_qk(P, A, qa_d[hi], ka_d[hi], 67)
            load_vaug(P, A, v_d[hi])
            attention_head(A, 0, 67, 0.125, blocks_A(0.0), None, oa_d, hi * 64, btab_h=hi)
    if _LBPARTS["gla"]:
        gla_head(P, A)
    return P.finalize()


def build_LD_prog():
    P = Prog()
    A = AttnCtx(P, len(MS_C) + len(MS_D), NS=6, LOOK=4)
    nc = P.nc
    masks_d = P.din("masks", [128, len(MS_C) + len(MS_D), TT], BF16)
    btab_d = P.din("btab", [128, 2, 20], F32)
    sink_d = P.din("sinks", [128, 2], F32)
    sq_d = [P.din(f"sw_qaT{i}", [67, S], BF16) for i in range(2)]
    sk_d = P.din("sw_kaT", [67, S], BF16)
    sv_d = P.din("sw_v", [128, NB, 128], BF16)
    mq_d = [P.din(f"m_qT{i}", [96, S], BF16) for i in range(2)]
    mk_d = [P.din(f"m_kT{i}", [96, S], BF16) for i in range(2)]
    mv_d = [P.din(f"m_v{i}", [128, NB, 128], BF16) for i in range(2)]
    oc_d = P.dout("ocT", [128, S], BF16)
    od_d = P.dout("odT", [128, S], BF16)
    P.dma("sp", A.masks[:], masks_d[:, :, :], [], [A.b_masks])
    P.dma("sp", A.btab[:], btab_d[:, :, :], [], [A.b_btab])
    esink = P.sb("esink", [128, 2], F32)
    b_es = P.buf("esink")
    P.dma("sp", esink[:], sink_d[:, :], [], [b_es])
    P.act(esink[:], esink[:], AF.Exp, [b_es], [b_es])
    load_vaug(P, A, sv_d)
    for hi in range(2):
        load_qk(P, A, sq_d[hi], sk_d if hi == 0 else None, 67)
        attention_head(A, 0, 67, 0.125, blocks_C(0.0), None, oc_d, hi * 64, den_add=(esink, b_es, hi), btab_h=hi)
    for hi in range(2):
        for i in range(4):
            cs_ = slice(i * 4096, (i + 1) * 4096)
            P.dma("sp", A.QT[0:96, cs_], mq_d[hi][:, cs_], [], [A.b_QT])
            P.dma("sp", A.KT[0:96, cs_], mk_d[hi][:, cs_], [], [A.b_KT])
        load_vaug(P, A, mv_d[hi])
        attention_head(A, 0, 96, 96.0 ** -0.5, blocks_D, None, od_d, hi * 64)
    return P.finalize()


_PROGS = {}
_DBG = {}
_RESUME = {}
_STOP = [None]


def _prog(name, fn):
    if name not in _PROGS:
        _PROGS[name] = fn()
    return _PROGS[name]


def _gl(g, kc):
    return np.ascontiguousarray(np.asarray(g, np.float32).reshape(kc, 128).T)


_TRACE = [False]
_NTILES = [None]
_TIMES = []


def _run(nc, ins):
    if _TRACE[0]:
        res = run_bass_kernel_spmd(nc, ins, core_ids=list(range(NCORES)), trace=True)
        _TIMES.append(res.exec_time_ns)
    else:
        res = run_bass_kernel_spmd(nc, ins, core_ids=list(range(NCORES)))
    return res.results


def _vaug(vT):
    v = np.ones((128, NB, 128), NPBF)
    v[:, :, 0:64] = vT.T.reshape(NB, 128, 64).transpose(1, 0, 2)
    return v


def _vblk(vT):
    return np.ascontiguousarray(vT.T.reshape(NB, 128, 128).transpose(1, 0, 2))


def _aug_rows(slope):
    k = np.arange(S)
    iq = k % TT
    kr = np.stack([np.ones(S), np.ones(S), (k % 128).astype(np.float64)]).astype(np.float32)
    qr = np.stack([-8.0 * slope * (iq % 256), -8.0 * slope * (iq - iq % 256), np.full(S, 8.0 * slope)]).astype(np.float32)
    return kr.astype(NPBF), qr.astype(NPBF)


def _alibi(n):
    return [2.0 ** (-8.0 * (i + 1) / n) for i in range(n)]


def _rope_tables(pos):
    r = np.arange(128)
    i = r % 32
    idx = i % 16
    freqs = (np.float32(10000.0) ** (-(idx.astype(np.float32)) / np.float32(16.0))).astype(np.float32)
    ang = pos.astype(np.float32)[None, :] * freqs[:, None]
    cos = np.cos(ang).astype(np.float32)
    sin = np.sin(ang).astype(np.float32)
    sgn = np.where(i < 16, -1.0, 1.0).astype(np.float32)[:, None]
    return np.ascontiguousarray(cos), np.ascontiguousarray(sin * sgn)


def _btab(slopes2):
    t = np.zeros((128, 2, 20), np.float32)
    for i, s_ in enumerate(slopes2):
        for mi in range(20):
            t[:, i, mi] = -s_ * 128.0 * (mi - 3)
    return t


def kernel(x, norm_mix_pre, norm_mix_post, norm_ffn_pre, norm_ffn_post,
           ffn_w_gate, ffn_w_up, ffn_w_down,
           ab_w_in, ab_w_out, gla_w_gate_up, gla_b_gate, gla_norm,
           cd_w_in, cd_w_out, swa_sinks, mla_q_norm, mla_w_uq, mla_kv_norm, mla_w_ukv):
    f32 = lambda a: np.ascontiguousarray(np.asarray(a, np.float32))
    x = f32(x)
    cores = [(c // 4, c % 4) for c in range(NCORES)]
    C = np.ascontiguousarray
    if "P0" in _RESUME:
        P0 = _RESUME["P0"]
    nc = _prog("LA", build_LA) if "P0" not in _RESUME else None
    ins = [{"xT": C(x[b, j * TSH:(j + 1) * TSH, :].T), "w_in": f32(ab_w_in[0]), "gpre": _gl(norm_mix_pre[0], 8)}
           for (b, j) in cores]
    r = _run(nc, ins) if nc is not None else None
    if r is not None:
      P0 = [np.concatenate([r[4 * b + j]["pT"] for j in range(4)], axis=1) for b in range(2)]
    _DBG['P0'] = P0
    nc = _prog("LB", build_LB_prog)
    slA = _alibi(8)
    masksA = make_masks(MS_A, mult_A)
    tri = np.zeros((128, 128), np.float32)
    for s_ in range(128):
        for c_ in range(128):
            if s_ <= c_ and s_ // 64 == c_ // 64:
                tri[s_, c_] = 1.0
    tri = tri.astype(NPBF)
    ident = np.eye(64, dtype=np.float32).astype(NPBF)
    ins = []
    for (b, hp) in cores:
        p = P0[b]
        d = {"masks": masksA, "btab": _btab([slA[2 * hp], slA[2 * hp + 1]])}
        for i in range(2):
            h = 2 * hp + i
            kr, qr = _aug_rows(slA[h])
            d[f"qaT{i}"] = C(np.concatenate([p[h * 64:(h + 1) * 64], qr], axis=0))
            d[f"kaT{i}"] = C(np.concatenate([p[512 + h * 64:512 + (h + 1) * 64], kr], axis=0))
            d[f"v{i}"] = _vaug(p[1024 + h * 64:1024 + (h + 1) * 64])
        gh = hp
        d["gq"] = C(p[1536 + gh * 64:1536 + (gh + 1) * 64])
        d["gk"] = C(p[1792 + gh * 64:1792 + (gh + 1) * 64])
        d["gv"] = _vblk(p[2048 + gh * 128:2048 + (gh + 1) * 128])
        d["gr"] = C(p[2560 + gh * 128:2560 + (gh + 1) * 128])
        d["gg"] = C(p[3072:3088])
        d["gwup"] = f32(gla_w_gate_up[0][:, gh * 64:(gh + 1) * 64])
        d["gnb"] = f32(gla_b_gate[0][gh * 64:(gh + 1) * 64][:, None])
        d["ggn"] = f32(gla_norm[0][:, None])
        d["gtri"] = tri
        d["gident"] = ident
        ins.append(d)
    r = _run(nc, ins)
    om = []
    for b in range(2):
        oa = np.concatenate([r[4 * b + hp]["oaT"] for hp in range(4)], axis=0)
        ob = np.concatenate([r[4 * b + hp]["obT"] for hp in range(4)], axis=0)
        om.append(np.concatenate([oa, ob], axis=0))
    _DBG['om'] = om
    if _STOP[0] == 'LB':
        return None
    nc = _prog("LE", lambda: build_LCE(False))
    ins = []
    for (b, j) in cores:
        ins.append({"xT": C(x[b, j * TSH:(j + 1) * TSH, :].T),
                    "m_omix": C(om[b][:, j * TSH:(j + 1) * TSH]), "m_wo": f32(ab_w_out[0]),
                    "m_gmpost": _gl(norm_mix_post[0], 8),
                    "f_wg": f32(ffn_w_gate[0]), "f_wu": f32(ffn_w_up[0]), "f_wd": f32(ffn_w_down[0]),
                    "f_gpre": _gl(norm_ffn_pre[0], 8), "f_gpost": _gl(norm_ffn_post[0], 8)})
    r = _run(nc, ins)
    x1T = [r[c]["xoT"] for c in range(NCORES)]
    nc = _prog("LC", lambda: build_LCE(True))
    w1 = f32(cd_w_in[0])
    w1x = np.concatenate([w1, w1[:, 1424:1440], w1[:, 1408:1424]], axis=1)
    wuq = f32(mla_w_uq[0]).reshape(384, 8, 96)
    wuq_r = np.concatenate([wuq[:, :, 0:64].reshape(384, 512), wuq[:, :, 64:96].reshape(384, 256),
                            np.concatenate([wuq[:, :, 80:96], wuq[:, :, 64:80]], axis=2).reshape(384, 256)], axis=1)
    ins = []
    for (b, j) in cores:
        cosT, sinT = _rope_tables(np.arange(j * TSH, (j + 1) * TSH))
        ins.append({"xT": C(x1T[4 * b + j]),
                    "w_in1": C(w1x), "gpre1": _gl(norm_mix_pre[1], 8), "w_uq": C(wuq_r),
                    "w_ukv": f32(mla_w_ukv[0]), "gq": _gl(mla_q_norm[0], 3), "gkv": _gl(mla_kv_norm[0], 2),
                    "cosT": cosT, "sinT": sinT})
    r = _run(nc, ins)
    P1 = [np.concatenate([r[4 * b + j]["p1T"] for j in range(4)], axis=1) for b in range(2)]
    _DBG['x1T'] = x1T
    _DBG['P1'] = P1
    nc = _prog("LD", build_LD_prog)
    slC = _alibi(8)
    swa_fn = lambda rel: ((rel >= 0) & (rel <= 127)).astype(np.float32)
    mla_fn = lambda rel: (rel >= 0).astype(np.float32)
    masksD = np.concatenate([make_masks(MS_C, swa_fn), make_masks(MS_D, mla_fn)], axis=1)
    sinks = f32(swa_sinks[0])
    ins = []
    for (b, hp) in cores:
        p = P1[b]
        kvh = hp // 2
        d = {"masks": C(masksD), "btab": _btab([slC[2 * hp], slC[2 * hp + 1]]),
             "sinks": C(np.tile(sinks[2 * hp:2 * hp + 2][None, :], (128, 1))),
             "sw_v": _vaug(p[640 + kvh * 64:640 + (kvh + 1) * 64])}
        for i in range(2):
            h = 2 * hp + i
            kr, qr = _aug_rows(slC[h])
            d[f"sw_qaT{i}"] = C(np.concatenate([p[h * 64:(h + 1) * 64], qr], axis=0))
            if i == 0:
                d["sw_kaT"] = C(np.concatenate([p[512 + kvh * 64:512 + (kvh + 1) * 64], kr], axis=0))
            d[f"m_qT{i}"] = C(np.concatenate([p[768 + h * 64:768 + (h + 1) * 64],
                                              p[1280 + h * 32:1280 + (h + 1) * 32]], axis=0))
            d[f"m_kT{i}"] = C(np.concatenate([p[1536 + h * 128:1536 + h * 128 + 64], p[2560:2592]], axis=0))
            d[f"m_v{i}"] = _vaug(p[1536 + h * 128 + 64:1536 + (h + 1) * 128])
        ins.append(d)
    r = _run(nc, ins)
    om1 = []
    for b in range(2):
        oc = np.concatenate([r[4 * b + hp]["ocT"] for hp in range(4)], axis=0)
        od = np.concatenate([r[4 * b + hp]["odT"] for hp in range(4)], axis=0)
        om1.append(np.concatenate([oc, od], axis=0))
    _DBG['om1'] = om1
    nc = _prog("LE", lambda: build_LCE(False))
    ins = []
    for c, (b, j) in enumerate(cores):
        ins.append({"xT": C(x1T[c]), "m_omix": C(om1[b][:, j * TSH:(j + 1) * TSH]), "m_wo": f32(cd_w_out[0]),
                    "m_gmpost": _gl(norm_mix_post[1], 8),
                    "f_wg": f32(ffn_w_gate[1]), "f_wu": f32(ffn_w_up[1]), "f_wd": f32(ffn_w_down[1]),
                    "f_gpre": _gl(norm_ffn_pre[1], 8), "f_gpost": _gl(norm_ffn_post[1], 8)})
    r = _run(nc, ins)
    out = np.empty((2, S, DM), np.float32)
    for c, (b, j) in enumerate(cores):
        out[b, j * TSH:(j + 1) * TSH, :] = r[c]["xoT"].T
    return out
```

```python
import contextlib
import numpy as np
import ml_dtypes
import concourse.bass as bass
import concourse.mybir as mybir
from concourse.bass_utils import run_bass_kernel_spmd

F32 = mybir.dt.float32
BF16 = mybir.dt.bfloat16
AF = mybir.ActivationFunctionType
ALU = mybir.AluOpType
NPBF = ml_dtypes.bfloat16

NCORES = 8
S = 16384
DM = 1024
TSH = 4096
TT = 512
TK = 256
DFF = 2816
EPS = 1e-6


class Buf:
    __slots__ = ("name", "lw", "rd", "rdd")

    def __init__(self, name=""):
        self.name = name
        self.lw = None
        self.rd = {}
        self.rdd = []


class Op:
    __slots__ = ("eng", "fn", "deps", "sig", "isdma", "idx", "sem", "val", "prev")

    def __init__(self, eng, fn, isdma):
        self.eng = eng
        self.fn = fn
        self.isdma = isdma
        self.deps = []
        self.sig = False
        self.sem = None
        self.val = 0
        self.prev = None


class Prog:
    EPOCH = 20000
    NDSEM = 8

    def __init__(self):
        self.nc = bass.Bass("TRN2", target_bir_lowering=False)
        self.es = contextlib.ExitStack()
        self.ops = []
        nc = self.nc
        self.eng = {"pe": nc.tensor, "act": nc.scalar, "dve": nc.vector,
                    "pool": nc.gpsimd, "sp": nc.sync}
        self.nbuf = 0

    def sb(self, name, shape, dt):
        return self.es.enter_context(self.nc.sbuf_tensor("s_" + name, list(shape), dt))

    def ps(self, name, shape=(128, 512), dt=F32):
        return self.es.enter_context(self.nc.psum_tensor("p_" + name, list(shape), dt))

    def din(self, name, shape, dt):
        return self.nc.dram_tensor(name, list(shape), dt, kind="ExternalInput").ap()

    def dout(self, name, shape, dt):
        return self.nc.dram_tensor(name, list(shape), dt, kind="ExternalOutput").ap()

    def buf(self, name=""):
        self.nbuf += 1
        return Buf(name or f"b{self.nbuf}")

    def bufs(self, n, name=""):
        return [self.buf(f"{name}{i}") for i in range(n)]

    def _record(self, o, reads, writes):
        deps = {}

        def add(d):
            if d is None or d is o:
                return
            deps[id(d)] = d

        for b in reads:
            add(b.lw)
        for b in writes:
            add(b.lw)
            for r in b.rd.values():
                add(r)
            for r in b.rdd:
                add(r)
        rawset = set(id(b.lw) for b in reads if b.lw is not None)
        best = {}
        out = []
        for d in deps.values():
            if d.isdma:
                out.append(d)
                continue
            if d.eng == o.eng and not o.isdma:
                if o.eng == "pe":
                    continue
            k = d.eng
            if k not in best or best[k].idx < d.idx:
                best[k] = d
        out.extend(best.values())
        for d in out:
            d.sig = True
        o.deps = out
        for b in reads:
            if o.isdma:
                b.rdd.append(o)
            else:
                b.rd[o.eng] = o
        for b in writes:
            b.lw = o
            b.rd = {}
            b.rdd = []
        o.idx = len(self.ops)
        self.ops.append(o)
        return o

    def op(self, eng, fn, reads=(), writes=()):
        return self._record(Op(eng, fn, False), list(reads), list(writes))

    def dma(self, q, out, in_, reads=(), writes=()):
        e = self.eng[q]
        o = Op(q, (lambda: e.dma_start(out=out, in_=in_)), True)
        o.sig = True
        return self._record(o, list(reads), list(writes))

    def mm(self, out, lhsT, rhs, start, stop, reads, writes):
        nc = self.nc
        return self.op("pe", lambda: nc.tensor.matmul(out, lhsT=lhsT, rhs=rhs, start=start, stop=stop,
                                                     skip_group_check=True), reads, writes)

    def act(self, out, in_, func, reads, writes, bias=0.0, scale=1.0):
        nc = self.nc
        return self.op("act", lambda: nc.scalar.activation(out=out, in_=in_, func=func, bias=bias, scale=scale),
                       reads, writes)

    def tt(self, out, in0, in1, op, reads, writes, eng="dve"):
        e = self.eng[eng]
        return self.op(eng, lambda: e.tensor_tensor(out=out, in0=in0, in1=in1, op=op), reads, writes)

    def stt(self, out, in0, scalar, in1, op0, op1, reads, writes, eng="dve"):
        e = self.eng[eng]
        return self.op(eng, lambda: e.scalar_tensor_tensor(out=out, in0=in0, scalar=scalar, in1=in1,
                                                          op0=op0, op1=op1), reads, writes)

    def ts(self, out, in0, s1, s2, op0, op1, reads, writes, eng="dve"):
        e = self.eng[eng]
        if s2 is None:
            return self.op(eng, lambda: e.tensor_scalar(out=out, in0=in0, scalar1=s1, scalar2=None, op0=op0),
                           reads, writes)
        return self.op(eng, lambda: e.tensor_scalar(out=out, in0=in0, scalar1=s1, scalar2=s2, op0=op0, op1=op1),
                       reads, writes)

    def copy(self, out, in_, reads, writes, eng="dve"):
        if eng == "act":
            nc = self.nc
            return self.op("act", lambda: nc.scalar.copy(out=out, in_=in_), reads, writes)
        e = self.eng[eng]
        return self.op(eng, lambda: e.tensor_copy(out=out, in_=in_), reads, writes)

    def recip(self, out, in_, reads, writes):
        nc = self.nc
        return self.op("dve", lambda: nc.vector.reciprocal(out=out, in_=in_), reads, writes)

    def memset(self, ap, val, writes, eng="dve"):
        e = self.eng[eng]
        return self.op(eng, lambda: e.memset(ap, val), [], writes)

    def finalize(self):
        nc = self.nc
        es = self.es
        cnt = {}
        sems = {}
        for o in self.ops:
            if o.isdma or not o.sig:
                continue
            c = cnt.get(o.eng, 0)
            ep = c // self.EPOCH
            key = (o.eng, ep)
            if key not in sems:
                sems[key] = es.enter_context(nc.semaphore(f"s_{o.eng}_{ep}"))
            o.sem = sems[key]
            o.val = c % self.EPOCH + 1
            cnt[o.eng] = c + 1
        dsem = {}
        dcnt = {}
        dn = {}
        for o in self.ops:
            if not o.isdma:
                continue
            n = dn.get(o.eng, 0)
            k = n % self.NDSEM
            key = (o.eng, k)
            if key not in dsem:
                dsem[key] = es.enter_context(nc.semaphore(f"d_{o.eng}_{k}"))
                dcnt[key] = 0
            o.sem = dsem[key]
            o.prev = dcnt[key]
            dcnt[key] += 16
            o.val = dcnt[key]
            dn[o.eng] = n + 1
        waited = {}
        for o in self.ops:
            e = self.eng[o.eng]
            w = waited.setdefault(o.eng, {})
            need = {}
            for d in o.deps:
                k = id(d.sem)
                if k not in need or need[k][1] < d.val:
                    need[k] = (d.sem, d.val)
            if o.isdma and o.prev > 0:
                k = id(o.sem)
                if k not in need or need[k][1] < o.prev:
                    need[k] = (o.sem, o.prev)
            for k, (sm, v) in need.items():
                if w.get(k, 0) >= v:
                    continue
                e.wait_ge(sm, v)
                w[k] = v
            ins = o.fn()
            if o.isdma:
                ins.then_inc(o.sem, 16)
            elif o.sig:
                ins.then_inc(o.sem, 1)
        for key, sm in dsem.items():
            nc.sync.wait_ge(sm, dcnt[key])
        self.es.close()
        return nc


class TokCtx:
    def __init__(self, P):
        self.P = P
        self.ones = P.sb("ones", [128, 128], BF16)
        self.b_ones = P.buf("ones")
        P.memset(self.ones[:], 1.0, [self.b_ones])
        self.sq = P.sb("sq", [128, 8, TK], BF16)
        self.b_sq = P.bufs(8, "sq")
        self.sv = P.sb("sv", [128, TK], F32)
        self.b_sv = P.buf("sv")
        self.rstd = P.sb("rstd", [128, TK], F32)
        self.b_rstd = P.buf("rstd")
        self.psb = [P.ps(f"psb{i}")[:, 0:TK] for i in range(8)]
        self.b_ps = P.bufs(8, "ps")
        self.yT = P.sb("yT", [128, 8, TK], F32)
        self.b_y = P.bufs(8, "y")
        self.tmp = P.sb("tmpf", [128, 2, TK], F32)
        self.b_tmp = P.bufs(2, "tmp")
        self.ntmp = 0
        self.nps = 0

    def next_ps(self):
        i = self.nps % 6
        self.nps += 1
        return i


def load_weight(P, W, Wsb, KC, buf, rows=None):
    for kc in range(KC):
        r0 = kc * 128
        r1 = r0 + 128 if rows is None else min(r0 + 128, rows)
        P.dma("pool", Wsb[0:r1 - r0, kc, :], W[r0:r1, :], [], [buf])


def load_gain(P, g_d, KC, name, scale):
    g = P.sb(name, [128, KC], F32)
    b = P.buf(name)
    P.dma("sp", g[:], g_d[:, :], [], [b])
    P.ts(g[:], g[:], float(scale), None, ALU.mult, None, [b], [b])
    return g, b


def stats_rstd(C, KC, n, sq_bufs):
    P = C.P
    bank = 6 + (C.ntmp % 2)
    C.ntmp += 1
    ps = C.psb[bank]
    bps = C.b_ps[bank]
    for kc in range(KC):
        P.mm(ps[:, :], C.ones[:, :], C.sq[:, kc, :], kc == 0, kc == KC - 1,
             [C.b_ones, sq_bufs[kc]], [bps])
    P.act(C.sv[:], ps[:, :], AF.Ln, [bps], [C.b_sv], bias=float(n * EPS))
    P.act(C.rstd[:], C.sv[:], AF.Exp, [C.b_sv], [C.b_rstd], scale=-0.5)


def norm_fm(C, xT, b_x, KC, n, g, b_g, hT, b_h):
    P = C.P
    for kc in range(KC):
        P.act(C.sq[:, kc, :], xT[:, kc, :], AF.Square, [b_x[kc]], [C.b_sq[kc]])
    stats_rstd(C, KC, n, C.b_sq)
    for kc in range(KC):
        P.stt(hT[:, kc, :], xT[:, kc, :], g[:, kc:kc + 1], C.rstd[:], ALU.mult, ALU.mult,
              [b_x[kc], b_g, C.b_rstd], [b_h[kc]])


def linear_chunk(C, Wsb, b_w, KC, col0, M, inT, b_in, ps, bps, krows=None):
    P = C.P
    for kc in range(KC):
        kr = 128 if krows is None else krows[kc]
        P.mm(ps[0:M, :], Wsb[0:kr, kc, col0:col0 + M], inT[0:kr, kc, :], kc == 0, kc == KC - 1,
             [b_w, b_in[kc]], [bps])


def postnorm_residual(C, produce, g, b_g, xT, b_x):
    P = C.P
    for oc in range(8):
        bi = C.next_ps()
        ps, bps = C.psb[bi], C.b_ps[bi]
        produce(oc, ps, bps)
        P.act(C.sq[:, oc, :], ps[:, :], AF.Square, [bps], [C.b_sq[oc]])
        P.copy(C.yT[:, oc, :], ps[:, :], [bps, C.b_sq[oc]], [C.b_y[oc]])
    stats_rstd(C, 8, DM, C.b_sq)
    for oc in range(8):
        ti = oc % 2
        P.stt(C.tmp[:, ti, :], C.yT[:, oc, :], g[:, oc:oc + 1], C.rstd[:], ALU.mult, ALU.mult,
              [C.b_y[oc], b_g, C.b_rstd], [C.b_tmp[ti]])
        P.tt(xT[:, oc, :], xT[:, oc, :], C.tmp[:, ti, :], ALU.add, [b_x[oc], C.b_tmp[ti]], [b_x[oc]])


class FFN:
    def __init__(self, C, pfx):
        P = C.P
        self.C = C
        self.wg_d = P.din(pfx + "wg", [DM, DFF], F32)
        self.wu_d = P.din(pfx + "wu", [DM, DFF], F32)
        self.wd_d = P.din(pfx + "wd", [DFF, DM], F32)
        self.gpre_d = P.din(pfx + "gpre", [128, 8], F32)
        self.gpost_d = P.din(pfx + "gpost", [128, 8], F32)

    def alloc(self, shared):
        C, P = self.C, self.C.P
        if shared is None:
            self.wg = P.sb("wg", [128, 8, DFF], BF16)
            self.wu = P.sb("wu", [128, 8, DFF], BF16)
            self.wd = P.sb("wd", [128, 22, DM], BF16)
            self.b_wg, self.b_wu, self.b_wd = P.buf("wg"), P.buf("wu"), P.buf("wd")
            self.aT = P.sb("aT", [128, 22, TK], BF16)
            self.b_a = P.bufs(22, "a")
            self.sg = P.sb("sg", [128, 2, TK], BF16)
            self.b_sg = P.bufs(2, "sg")
            self.sgf = P.sb("sgf", [128, 2, TK], F32)
            self.b_sgf = P.bufs(2, "sgf")
        else:
            for k in ("wg", "wu", "wd", "b_wg", "b_wu", "b_wd", "aT", "b_a", "sg", "b_sg", "sgf", "b_sgf"):
                setattr(self, k, getattr(shared, k))

    def load(self, tag):
        C, P = self.C, self.C.P
        load_weight(P, self.wg_d, self.wg, 8, self.b_wg)
        load_weight(P, self.wu_d, self.wu, 8, self.b_wu)
        load_weight(P, self.wd_d, self.wd, 22, self.b_wd)
        self.gpre, self.b_gpre = load_gain(P, self.gpre_d, 8, tag + "gpre", 32.0)
        self.gpost, self.b_gpost = load_gain(P, self.gpost_d, 8, tag + "gpost", 32.0)

    def tile(self, xT, b_x, hT, b_h):
        C, P = self.C, self.C.P
        norm_fm(C, xT, b_x, 8, DM, self.gpre, self.b_gpre, hT, b_h)
        for hc in range(22):
            gi = C.next_ps()
            ui = C.next_ps()
            linear_chunk(C, self.wg, self.b_wg, 8, hc * 128, 128, hT, b_h, C.psb[gi], C.b_ps[gi])
            linear_chunk(C, self.wu, self.b_wu, 8, hc * 128, 128, hT, b_h, C.psb[ui], C.b_ps[ui])
            si = hc % 2
            P.act(self.sgf[:, si, :], C.psb[gi][:, :], AF.Tanh, [C.b_ps[gi]], [self.b_sgf[si]], scale=0.5)
            P.stt(self.sgf[:, si, :], self.sgf[:, si, :], 1.0, C.psb[gi][:, :], ALU.add, ALU.mult,
                  [self.b_sgf[si], C.b_ps[gi]], [self.b_sgf[si]])
            P.stt(self.aT[:, hc, :], self.sgf[:, si, :], 0.5, C.psb[ui][:, :], ALU.mult, ALU.mult,
                  [self.b_sgf[si], C.b_ps[ui]], [self.b_a[hc]])

        def produce(oc, ps, bps):
            for hc in range(22):
                P.mm(ps[:, :], self.wd[:, hc, oc * 128:(oc + 1) * 128], self.aT[:, hc, :],
                     hc == 0, hc == 21, [self.b_wd, self.b_a[hc]], [bps])
        postnorm_residual(C, produce, self.gpost, self.b_gpost, xT, b_x)


class MixOut:
    def __init__(self, C, pfx):
        P = C.P
        self.C = C
        self.wo_d = P.din(pfx + "wo", [DM, DM], F32)
        self.gpost_d = P.din(pfx + "gmpost", [128, 8], F32)
        self.o_d = P.din(pfx + "omix", [DM, TSH], BF16)
        self.wo = P.sb("wo", [128, 8, DM], BF16)
        self.b_wo = P.buf("wo")
        self.oT = P.sb("oT", [128, 8, TK], BF16)
        self.b_o = P.bufs(8, "o")

    def load(self, tag):
        P = self.C.P
        load_weight(P, self.wo_d, self.wo, 8, self.b_wo)
        self.gpost, self.b_gpost = load_gain(P, self.gpost_d, 8, tag + "gmpost", 32.0)

    def tile(self, t, xT, b_x):
        C, P = self.C, self.C.P
        o_v = self.o_d.rearrange("(kc p) t -> p kc t", p=128)
        P.dma("sp", self.oT[:, :, :], o_v[:, :, t * TK:(t + 1) * TK], [], self.b_o)

        def produce(oc, ps, bps):
            linear_chunk(C, self.wo, self.b_wo, 8, oc * 128, 128, self.oT, self.b_o, ps, bps)
        postnorm_residual(C, produce, self.gpost, self.b_gpost, xT, b_x)


def emit_out_chunk(C, ps, bps, M, out_d, row0, t, stage, b_stage, si):
    P = C.P
    eng = "act" if si % 2 == 0 else "dve"
    P.copy(stage[0:M, si, :], ps[0:M, :], [bps], [b_stage[si]], eng=eng)
    P.dma("sp", out_d[row0:row0 + M, t * TK:(t + 1) * TK], stage[0:M, si, :], [b_stage[si]], [])


L0_IN = 3088


def build_LA():
    global TK
    TK = 512
    try:
        return _build_LA()
    finally:
        TK = 256


def _build_LA():
    P = Prog()
    C = TokCtx(P)
    xT_d = P.din("xT", [DM, TSH], F32)
    w_d = P.din("w_in", [DM, L0_IN], F32)
    g_d = P.din("gpre", [128, 8], F32)
    out_d = P.dout("pT", [L0_IN, TSH], BF16)
    w = P.sb("w_in", [128, 8, L0_IN], BF16)
    b_w = P.buf("w")
    load_weight(P, w_d, w, 8, b_w)
    g, b_g = load_gain(P, g_d, 8, "gpre_s", 32.0)
    xT = P.sb("xTs", [128, 8, TK], F32)
    b_x = P.bufs(8, "x")
    hT = P.sb("hT", [128, 8, TK], BF16)
    b_h = P.bufs(8, "h")
    stage = P.sb("stage", [128, 4, TK], BF16)
    b_st = P.bufs(4, "st")
    x_v = xT_d.rearrange("(kc p) t -> p kc t", p=128)
    nst = 0
    for t in range(TSH // TK):
        for kc in range(8):
            P.dma("sp", xT[:, kc, :], x_v[:, kc, t * TK:(t + 1) * TK], [], [b_x[kc]])
        norm_fm(C, xT, b_x, 8, DM, g, b_g, hT, b_h)
        for oc in range(25):
            M = 128 if oc < 24 else 16
            bi = C.next_ps()
            linear_chunk(C, w, b_w, 8, oc * 128, M, hT, b_h, C.psb[bi], C.b_ps[bi])
            emit_out_chunk(C, C.psb[bi], C.b_ps[bi], M, out_d, oc * 128, t, stage, b_st, nst % 4)
            nst += 1
    return P.finalize()


L1_W = 1440 + 32
UQ_W = 512 + 256 + 256
P1_ROWS = 512 + 128 + 128 + 512 + 256 + 1024 + 32


def build_LCE(with_l1):
    global TK
    TK = 512 if with_l1 else 256
    try:
        return _build_LCE(with_l1)
    finally:
        TK = 256


def _build_LCE(with_l1):
    P = Prog()
    C = TokCtx(P)
    xT_d = P.din("xT", [DM, TSH], F32)
    xo_d = None if with_l1 else P.dout("xoT", [DM, TSH], F32)
    if not with_l1:
        mo = MixOut(C, "m_")
        ff = FFN(C, "f_")
        ff.alloc(None)
        mo.load("m")
        ff.load("f")
    xT = P.sb("xTs", [128, 8, TK], F32)
    b_x = P.bufs(8, "x")
    hT = P.sb("hT", [128, 8, TK], BF16)
    b_h = P.bufs(8, "h")
    x_v = xT_d.rearrange("(kc p) t -> p kc t", p=128)
    xo_v = None if with_l1 else xo_d.rearrange("(kc p) t -> p kc t", p=128)
    if with_l1:
        w1_d = P.din("w_in1", [DM, L1_W], F32)
        g1_d = P.din("gpre1", [128, 8], F32)
        wuq_d = P.din("w_uq", [384, UQ_W], F32)
        wukv_d = P.din("w_ukv", [256, 1024], F32)
        gq_d = P.din("gq", [128, 3], F32)
        gkv_d = P.din("gkv", [128, 2], F32)
        cos_d = P.din("cosT", [128, TSH], F32)
        sin_d = P.din("sinT", [128, TSH], F32)
        p1_d = P.dout("p1T", [P1_ROWS, TSH], BF16)
        w1 = P.sb("w_in1", [128, 8, L1_W], BF16)
        b_w1 = P.buf("w1")
        wuq = P.sb("w_uq", [128, 3, UQ_W], BF16)
        b_wuq = P.buf("wuq")
        wukv = P.sb("w_ukv", [128, 2, 1024], BF16)
        b_wukv = P.buf("wukv")
        load_weight(P, w1_d, w1, 8, b_w1)
        load_weight(P, wuq_d, wuq, 3, b_wuq)
        load_weight(P, wukv_d, wukv, 2, b_wukv)
        g1, b_g1 = load_gain(P, g1_d, 8, "g1s", 32.0)
        gq, b_gq = load_gain(P, gq_d, 3, "gqs", float(np.sqrt(384.0)))
        gkv, b_gkv = load_gain(P, gkv_d, 2, "gkvs", 16.0)
        stage = P.sb("stage", [128, 4, TK], BF16)
        b_st = P.bufs(4, "st")
        cT = P.sb("cT", [128, 5, TK], F32)
        b_c = P.bufs(5, "c")
        cn = P.sb("cn", [128, 5, TK], BF16)
        b_cn = P.bufs(5, "cn")
        cs = P.sb("cs", [128, TK], F32)
        sn = P.sb("sn", [128, TK], F32)
        b_cs, b_sn = P.buf("cs"), P.buf("sn")
        rt = P.sb("rt", [128, 2, TK], F32)
        b_rt = P.bufs(2, "rt")
    nst = 0
    for t in range(TSH // TK if _NTILES[0] is None else _NTILES[0]):
        for kc in range(8):
            P.dma("sp", xT[:, kc, :], x_v[:, kc, t * TK:(t + 1) * TK], [], [b_x[kc]])
        if not with_l1:
            mo.tile(t, xT, b_x)
            ff.tile(xT, b_x, hT, b_h)
            for kc in range(8):
                P.dma("sp", xo_v[:, kc, t * TK:(t + 1) * TK], xT[:, kc, :], [b_x[kc]], [])
            continue
        norm_fm(C, xT, b_x, 8, DM, g1, b_g1, hT, b_h)
        P.dma("sp", cs[:], cos_d[:, t * TK:(t + 1) * TK], [], [b_cs])
        P.dma("sp", sn[:], sin_d[:, t * TK:(t + 1) * TK], [], [b_sn])
        for oc in range(6):
            bi = C.next_ps()
            linear_chunk(C, w1, b_w1, 8, oc * 128, 128, hT, b_h, C.psb[bi], C.b_ps[bi])
            emit_out_chunk(C, C.psb[bi], C.b_ps[bi], 128, p1_d, oc * 128, t, stage, b_st, nst % 4)
            nst += 1
        for i in range(5):
            bi = C.next_ps()
            linear_chunk(C, w1, b_w1, 8, 768 + i * 128, 128, hT, b_h, C.psb[bi], C.b_ps[bi])
            P.copy(cT[:, i, :], C.psb[bi][:, :], [C.b_ps[bi]], [b_c[i]])
        bi, bj = C.next_ps(), C.next_ps()
        linear_chunk(C, w1, b_w1, 8, 1408, 32, hT, b_h, C.psb[bi], C.b_ps[bi])
        linear_chunk(C, w1, b_w1, 8, 1440, 32, hT, b_h, C.psb[bj], C.b_ps[bj])
        P.tt(rt[0:32, 0, :], C.psb[bi][0:32, :], cs[0:32, :], ALU.mult, [C.b_ps[bi], b_cs], [b_rt[0]])
        P.tt(rt[0:32, 1, :], C.psb[bj][0:32, :], sn[0:32, :], ALU.mult, [C.b_ps[bj], b_sn], [b_rt[1]])
        si = nst % 4
        nst += 1
        P.tt(stage[0:32, si, :], rt[0:32, 0, :], rt[0:32, 1, :], ALU.add, [b_rt[0], b_rt[1]], [b_st[si]])
        P.dma("sp", p1_d[2560:2592, t * TK:(t + 1) * TK], stage[0:32, si, :], [b_st[si]], [])
        for (c0, kcn, n, g, b_g) in ((0, 3, 384, gq, b_gq), (3, 2, 256, gkv, b_gkv)):
            for kc in range(kcn):
                P.act(C.sq[:, kc, :], cT[:, c0 + kc, :], AF.Square, [b_c[c0 + kc]], [C.b_sq[kc]])
            stats_rstd(C, kcn, n, C.b_sq)
            for kc in range(kcn):
                P.stt(cn[:, c0 + kc, :], cT[:, c0 + kc, :], g[:, kc:kc + 1], C.rstd[:], ALU.mult, ALU.mult,
                      [b_c[c0 + kc], b_g, C.b_rstd], [b_cn[c0 + kc]])
        for oc in range(4):
            bi = C.next_ps()
            linear_chunk(C, wuq, b_wuq, 3, oc * 128, 128, cn, b_cn[0:3], C.psb[bi], C.b_ps[bi])
            emit_out_chunk(C, C.psb[bi], C.b_ps[bi], 128, p1_d, 768 + oc * 128, t, stage, b_st, nst % 4)
            nst += 1
        for oc in range(2):
            bi, bj = C.next_ps(), C.next_ps()
            linear_chunk(C, wuq, b_wuq, 3, 512 + oc * 128, 128, cn, b_cn[0:3], C.psb[bi], C.b_ps[bi])
            linear_chunk(C, wuq, b_wuq, 3, 768 + oc * 128, 128, cn, b_cn[0:3], C.psb[bj], C.b_ps[bj])
            P.tt(rt[:, 0, :], C.psb[bi][:, :], cs[:], ALU.mult, [C.b_ps[bi], b_cs], [b_rt[0]])
            P.tt(rt[:, 1, :], C.psb[bj][:, :], sn[:], ALU.mult, [C.b_ps[bj], b_sn], [b_rt[1]])
            si = nst % 4
            nst += 1
            P.tt(stage[:, si, :], rt[:, 0, :], rt[:, 1, :], ALU.add, [b_rt[0], b_rt[1]], [b_st[si]])
            P.dma("sp", p1_d[1280 + oc * 128:1408 + oc * 128, t * TK:(t + 1) * TK], stage[:, si, :],
                  [b_st[si]], [])
        cnkv = cn[:, 3:5, :]
        for oc in range(8):
            bi = C.next_ps()
            P2 = C.P
            for kc in range(2):
                P2.mm(C.psb[bi][:, :], wukv[:, kc, oc * 128:(oc + 1) * 128], cn[:, 3 + kc, :],
                      kc == 0, kc == 1, [b_wukv, b_cn[3 + kc]], [C.b_ps[bi]])
            emit_out_chunk(C, C.psb[bi], C.b_ps[bi], 128, p1_d, 1536 + oc * 128, t, stage, b_st, nst % 4)
            nst += 1
    return P.finalize()


NB = S // 128
NQT = S // TT
_MAXT = [NQT]


class AttnCtx:
    def __init__(self, P, nmask, NS=4, LOOK=2):
        self.P = P
        self.KT = P.sb("KT", [128, S], BF16)
        self.QT = P.sb("QT", [128, S], BF16)
        self.VA = P.sb("VA", [128, NB, 128], BF16)
        self.b_KT, self.b_QT, self.b_VA = P.buf("KT"), P.buf("QT"), P.buf("VA")
        self.masks = P.sb("masks", [128, nmask, TT], BF16)
        self.b_masks = P.buf("masks")
        self.kaug = P.sb("kaug", [3, 128], BF16)
        self.qaug = P.sb("qaug", [3, 2, TT], BF16)
        self.b_kaug, self.b_qaug = P.buf("kaug"), P.buf("qaug")
        self.btab = P.sb("btab", [128, 2, 20], F32)
        self.b_btab = P.buf("btab")
        self.NS = NS
        self.LOOK = LOOK
        self.ps_s = [P.ps(f"ps_s{i}") for i in range(self.NS)]
        self.b_ps_s = P.bufs(self.NS, "ps_s")
        self.ps_acc = [P.ps(f"ps_acc{i}") for i in range(2)]
        self.b_ps_acc = P.bufs(2, "ps_acc")
        self.pe_ = P.sb("pexp", [128, self.NS, TT], BF16)
        self.b_pe = P.bufs(self.NS, "pexp")
        self.pm = P.sb("pm", [128, self.NS, TT], BF16)
        self.b_pm = P.bufs(self.NS, "pm")
        self.rden = P.sb("rden", [128, 2, TT], F32)
        self.b_rden = P.bufs(2, "rden")
        self.ost = P.sb("ost", [64, 2, TT], BF16)
        self.b_ost = P.bufs(2, "ost")
        self.ntile = 0
        self.nblk = 0


def attention_head(A, krow0, K, scale, blocks_of_tile, aug_h, out_d, orow0, den_add=None, btab_h=None):
    P = A.P
    work = []
    for t in range(min(NQT, _MAXT[0])):
        bl = blocks_of_tile(t)
        for i, b in enumerate(bl):
            work.append((t, i == 0, i == len(bl) - 1) + tuple(b))
    LOOK = A.LOOK

    def emit_S(w):
        t, first, last, kb, qlo, qhi, mi, bias = w
        si = A.nblk % A.NS
        A.nblk += 1
        ps, bps = A.ps_s[si], A.b_ps_s[si]
        q0 = t * TT
        P.mm(ps[:, qlo:qhi], A.KT[krow0:krow0 + K, kb * 128:(kb + 1) * 128],
             A.QT[krow0:krow0 + K, q0 + qlo:q0 + qhi], True, aug_h is None, [A.b_KT, A.b_QT], [bps])
        if aug_h is not None:
            P.mm(ps[:, qlo:qhi], A.kaug[0:3, :], A.qaug[0:3, aug_h, qlo:qhi], False, True,
                 [A.b_kaug, A.b_qaug], [bps])
        if btab_h is not None:
            bias_ap = A.btab[:, btab_h, bias:bias + 1]
            P.op("act", lambda: P.nc.scalar.activation(out=A.pe_[:, si, qlo:qhi], in_=ps[:, qlo:qhi], func=AF.Exp,
                                                       bias=bias_ap, scale=float(scale)),
                 [bps, A.b_btab], [A.b_pe[si]])
        else:
            P.act(A.pe_[:, si, qlo:qhi], ps[:, qlo:qhi], AF.Exp, [bps], [A.b_pe[si]], bias=0.0,
                  scale=float(scale))
        if mi is not None:
            P.tt(A.pm[:, si, qlo:qhi], A.pe_[:, si, qlo:qhi], A.masks[:, mi, qlo:qhi], ALU.mult,
                 [A.b_pe[si], A.b_masks], [A.b_pm[si]])
            return (A.pm, A.b_pm[si], si)
        return (A.pe_, A.b_pe[si], si)

    pend = []
    for wi in range(len(work) + LOOK):
        if wi < len(work):
            pend.append((work[wi], emit_S(work[wi])))
        if wi >= LOOK:
            w, (pt, bpt, si) = pend.pop(0)
            t, first, last, kb, qlo, qhi, mi, bias = w
            ai = (A.ntile + t) % 2
            acc, bacc = A.ps_acc[ai], A.b_ps_acc[ai]
            P.mm(acc[:, qlo:qhi], A.VA[:, kb, :], pt[:, si, qlo:qhi], first, last, [A.b_VA, bpt], [bacc])
            if last:
                if den_add is not None:
                    dt_, db_, dc_ = den_add
                    P.ts(A.rden[64:128, ai, :], acc[64:128, :], dt_[64:128, dc_:dc_ + 1], None, ALU.add, None,
                         [bacc, db_], [A.b_rden[ai]])
                    P.recip(A.rden[64:128, ai, :], A.rden[64:128, ai, :], [A.b_rden[ai]], [A.b_rden[ai]])
                else:
                    P.recip(A.rden[64:128, ai, :], acc[64:128, :], [bacc], [A.b_rden[ai]])
                P.tt(A.ost[:, ai, :], acc[0:64, :], A.rden[64:128, ai, :], ALU.mult,
                     [bacc, A.b_rden[ai]], [A.b_ost[ai]])
                P.dma("sp", out_d[orow0:orow0 + 64, t * TT:(t + 1) * TT], A.ost[:, ai, :], [A.b_ost[ai]], [])
    A.ntile += NQT


def load_vaug(P, A, v_d):
    for i in range(4):
        P.dma("sp", A.VA[:, i * 32:(i + 1) * 32, :], v_d[:, i * 32:(i + 1) * 32, :], [], [A.b_VA])


def mult_A(rel):
    ok = lambda c: c.astype(np.float32)
    m = ok((rel >= 0) & (rel <= 128)) + ok((rel >= 0) & (rel <= 512) & (rel % 4 == 0)) \
        + ok((rel >= 0) & (rel <= 2048) & (rel % 16 == 0))
    return m


def make_masks(ms, fn):
    jk = np.arange(128)[:, None]
    iq = np.arange(TT)[None, :]
    out = np.zeros((128, len(ms), TT), np.float32)
    for i, m in enumerate(ms):
        out[:, i, :] = fn(128 * m + iq - jk)
    return out.astype(NPBF)


def make_aug(slopes):
    jk = np.arange(128, dtype=np.float32)
    iq = np.arange(TT)
    kaug = np.stack([np.ones(128, np.float32), np.ones(128, np.float32), jk]).astype(NPBF)
    qa = np.zeros((3, len(slopes), TT), np.float32)
    for i, s in enumerate(slopes):
        lo = (iq % 256).astype(np.float32)
        hi = (iq - iq % 256).astype(np.float32)
        qa[0, i] = -8.0 * s * lo
        qa[1, i] = -8.0 * s * hi
        qa[2, i] = 8.0 * s
    return kaug, qa.astype(NPBF)


MS_A = list(range(-3, 17))
MS_C = list(range(-3, 2))
MS_D = list(range(-3, 1))


def trim(m):
    qlo = max(0, -128 * m)
    return qlo


def build_LB(slopes_by_core_is_data=True):
    P = Prog()
    A = AttnCtx(P, len(MS_A), NS=5, LOOK=3)
    qa_d = [P.din(f"qaT{i}", [67, S], BF16) for i in range(2)]
    ka_d = [P.din(f"kaT{i}", [67, S], BF16) for i in range(2)]
    v_d = [P.din(f"v{i}", [128, NB, 128], BF16) for i in range(2)]
    masks_d = P.din("masks", [128, len(MS_A), TT], BF16)
    oa_d = P.dout("oaT", [128, S], BF16)
    P.dma("sp", A.masks[:], masks_d[:, :, :], [], [A.b_masks])
    btab_d = P.din("btab", [128, 2, 20], F32)
    P.dma("sp", A.btab[:], btab_d[:, :, :], [], [A.b_btab])
    return P, A, (qa_d, ka_d, v_d), oa_d


def load_qk(P, A, q_d, k_d, rows):
    for i in range(4):
        cs_ = slice(i * 4096, (i + 1) * 4096)
        if k_d is not None:
            P.dma("sp", A.KT[0:rows, cs_], k_d[:, cs_], [], [A.b_KT])
        if q_d is not None:
            P.dma("sp", A.QT[0:rows, cs_], q_d[:, cs_], [], [A.b_QT])


def blocks_A(slope):
    def f(t):
        out = []
        for kb in range(max(0, 4 * t - 16), 4 * t + 4):
            m = 4 * t - kb
            out.append((kb, trim(m), TT, m + 3, m + 3))
        return out
    return f


def blocks_C(slope):
    def f(t):
        out = []
        for kb in range(max(0, 4 * t - 1), 4 * t + 4):
            m = 4 * t - kb
            qlo = trim(m)
            qhi = min(TT, 128 * (2 - m)) if m <= 1 else TT
            out.append((kb, qlo, qhi, m + 3, m + 3))
        return out
    return f


def blocks_D(t):
    out = []
    for kb in range(0, 4 * t + 4):
        m = 4 * t - kb
        out.append((kb, trim(m), TT, (5 + m + 3) if m <= 0 else None, 0))
    return out


def gla_head(P, A):
    nc = P.nc
    q_d = P.din("gq", [64, S], BF16)
    k_d = P.din("gk", [64, S], BF16)
    r_d = P.din("gr", [128, S], BF16)
    g_d = P.din("gg", [16, S], BF16)
    v_d = P.din("gv", [128, NB, 128], BF16)
    wup_d = P.din("gwup", [16, 64], F32)
    nb_d = P.din("gnb", [64, 1], F32)
    gn_d = P.din("ggn", [128, 1], F32)
    tri_d = P.din("gtri", [128, 128], BF16)
    ob_d = P.dout("obT", [128, S], BF16)

    wup = P.sb("gwup", [16, 64], BF16)
    b_wup = P.buf()
    P.dma("pool", wup[:], wup_d[:, :], [], [b_wup])
    negb = P.sb("gnegb", [64, 1], F32)
    b_negb = P.buf()
    P.dma("sp", negb[:], nb_d[:, :], [], [b_negb])
    P.ts(negb[:], negb[:], -1.0, None, ALU.mult, None, [b_negb], [b_negb])
    gn = P.sb("ggn", [128, 1], F32)
    b_gn = P.buf()
    P.dma("sp", gn[:], gn_d[:, :], [], [b_gn])
    P.ts(gn[:], gn[:], float(0.5 * np.sqrt(128.0)), None, ALU.mult, None, [b_gn], [b_gn])
    tri = P.sb("gtri", [128, 128], BF16)
    b_tri = P.buf()
    P.dma("sp", tri[:], tri_d[:, :], [], [b_tri])
    ident = P.sb("gident", [64, 64], BF16)
    b_ident = P.buf()
    ident_d = P.din("gident", [64, 64], BF16)
    P.dma("sp", ident[:], ident_d[:, :], [], [b_ident])
    ones = P.sb("gones", [128, 128], BF16)
    b_ones = P.buf()
    P.memset(ones[:], 1.0, [b_ones])

    qk = P.sb("gqk", [64, 2, TT], BF16)
    b_qk = P.buf()
    g_t = P.sb("gg_t", [16, TT], BF16)
    b_g = P.buf()
    e_t = P.sb("ge", [64, TT], F32)
    b_e = P.buf()
    cab = [P.sb(f"gc{i}", [64, 8, 96], F32) for i in range(2)]
    b_ca = P.bufs(2, "gc")
    for i in range(2):
        P.memset(cab[i][:, :, 0:32], 0.0, [b_ca[i]])
    d1 = P.sb("gd1", [64, 8, 64], F32)
    d4 = P.sb("gd4", [64, 8, 64], F32)
    b_d1, b_d4 = P.buf(), P.buf()
    E = P.sb("gE", [64, 4, TT], F32)
    b_E = P.bufs(4, "gE")
    r_t2 = [P.sb(f"gr_t{i}", [128, TT], BF16) for i in range(2)]
    b_r2 = P.bufs(2, "gr")
    v_t2 = [P.sb(f"gv_t{i}", [128, 4, 128], BF16) for i in range(2)]
    b_v2 = P.bufs(2, "gv")
    dec2 = [P.sb(f"gdec{i}", [64, 8], F32) for i in range(2)]
    b_dec2 = P.bufs(2, "gdec")
    QK2 = [P.sb(f"gQK{i}", [64, 4, TT], BF16) for i in range(2)]
    b_QK2 = [P.bufs(4, f"gQK{i}_") for i in range(2)]
    K42 = [P.sb(f"gK4{i}", [128, 4, 64], BF16) for i in range(2)]
    b_K42 = P.bufs(2, "gK4")
    attm = P.sb("gattm", [128, 2, 128], BF16)
    b_attm = P.bufs(2, "gattm")
    St = P.sb("gS", [64, 128], F32)
    Sb = P.sb("gSb", [64, 128], BF16)
    b_S, b_Sb = P.buf(), P.buf()
    P.memset(St[:], 0.0, [b_S])
    P.memset(Sb[:], 0.0, [b_Sb])
    sq = P.sb("gsq", [128, TT], BF16)
    oT = P.sb("goT", [128, TT], F32)
    sv = P.sb("gsv", [128, TT], F32)
    rs = P.sb("grs", [128, TT], F32)
    sr = P.sb("gsr", [128, TT], F32)
    ot = P.sb("got", [128, TT], BF16)
    b_sq, b_oT, b_sv, b_rs, b_sr, b_ot = (P.buf() for _ in range(6))
    tr_ps = P.ps("gtr_ps", [128, 4, 64], BF16)
    b_tr = P.buf()
    ss_ps, b_ss = A.ps_s[4], A.b_ps_s[4]
    z_ps, b_z = A.ps_s[0], A.b_ps_s[0]
    o_ps, b_o = A.ps_s[1], A.b_ps_s[1]
    att_ps, b_att = A.ps_s[2], A.b_ps_s[2]
    dS_ps, b_dS = A.ps_s[3], A.b_ps_s[3]
    v_v = v_d
    NT = min(NQT, _MAXT[0])

    def setup(t):
        par = t % 2
        c0 = t * TT
        r_t, b_r, v_t, b_v = r_t2[par], b_r2[par], v_t2[par], b_v2[par]
        dec, b_dec, QK, b_QK, K4, b_K4 = dec2[par], b_dec2[par], QK2[par], b_QK2[par], K42[par], b_K42[par]
        P.dma("sp", qk[:, 0, :], q_d[:, c0:c0 + TT], [], [b_qk])
        P.dma("sp", qk[:, 1, :], k_d[:, c0:c0 + TT], [], [b_qk])
        P.dma("sp", r_t[:], r_d[:, c0:c0 + TT], [], [b_r])
        P.dma("sp", g_t[:], g_d[:, c0:c0 + TT], [], [b_g])
        P.dma("sp", v_t[:], v_v[:, t * 4:(t + 1) * 4, :], [], [b_v])
        P.mm(z_ps[0:64, :], wup[0:16, :], g_t[0:16, :], True, True, [b_wup, b_g], [b_z])
        P.op("act", lambda: nc.scalar.activation(out=e_t[:], in_=z_ps[0:64, :], func=AF.Exp,
                                                 bias=negb[:, 0:1], scale=-1.0), [b_z, b_negb], [b_e])
        P.act(cab[0][:, :, 32:96], e_t[:].rearrange("p (c s) -> p c s", s=64), AF.Ln, [b_e], [b_ca[0]], bias=1.0)
        cur = 0
        for s_ in (1, 2, 4, 8, 16, 32):
            src, dst = cab[cur], cab[1 - cur]
            P.tt(dst[:, :, 32:96], src[:, :, 32:96], src[:, :, 32 - s_:96 - s_], ALU.add,
                 [b_ca[cur]], [b_ca[1 - cur]])
            cur = 1 - cur
        bcs, b_bcs = cab[cur], b_ca[cur]
        P.tt(d1[:], bcs[:, :, 32:96], bcs[:, :, 63:64].to_broadcast([64, 8, 64]), ALU.subtract, [b_bcs], [b_d1])
        P.tt(d4[:], bcs[:, :, 32:96], bcs[:, :, 95:96].to_broadcast([64, 8, 64]), ALU.subtract, [b_bcs], [b_d4])
        d1f = d1[:].rearrange("p c s -> p (c s)")
        d4f = d4[:].rearrange("p c s -> p (c s)")
        P.act(E[:, 0, :], d1f, AF.Exp, [b_d1], [b_E[0]], scale=-1.0 / 16)
        P.act(E[:, 1, :], d1f, AF.Exp, [b_d1], [b_E[1]], scale=1.0 / 16)
        P.act(E[:, 2, :].rearrange("p (c s) -> p c s", s=64), bcs[:, :, 32:96], AF.Exp, [b_bcs], [b_E[2]],
              scale=-1.0 / 16)
        P.act(E[:, 3, :], d4f, AF.Exp, [b_d4], [b_E[3]], scale=1.0 / 16)
        P.act(dec[:, :], bcs[:, :, 95], AF.Exp, [b_bcs], [b_dec], scale=-1.0 / 16)
        P.stt(QK[:, 0, :], qk[:, 0, :], 0.125, E[:, 0, :], ALU.mult, ALU.mult, [b_qk, b_E[0]], [b_QK[0]])
        P.tt(QK[:, 1, :], qk[:, 1, :], E[:, 1, :], ALU.mult, [b_qk, b_E[1]], [b_QK[1]])
        P.stt(QK[:, 2, :], qk[:, 0, :], 0.125, E[:, 2, :], ALU.mult, ALU.mult, [b_qk, b_E[2]], [b_QK[2]])
        P.tt(QK[:, 3, :], qk[:, 1, :], E[:, 3, :], ALU.mult, [b_qk, b_E[3]], [b_QK[3]])
        for blk in range(4):
            P.op("pe", lambda blk=blk, QK=QK: nc.tensor.transpose(tr_ps[:, blk, :],
                                                                  QK[:, 3, blk * 128:(blk + 1) * 128],
                                                                  ident[:, :]), [b_QK[3], b_ident], [b_tr])
        P.copy(K4[:], tr_ps[:], [b_tr], [b_K4], eng="act")

    def chunks(t):
        par = t % 2
        v_t, b_v = v_t2[par], b_v2[par]
        dec, b_dec, QK, b_QK, K4, b_K4 = dec2[par], b_dec2[par], QK2[par], b_QK2[par], K42[par], b_K42[par]
        for blk in range(4):
            bs = slice(blk * 128, (blk + 1) * 128)
            ai = blk % 2
            P.mm(att_ps[:, bs], QK[:, 1, bs], QK[:, 0, bs], True, True, [b_QK[1], b_QK[0]], [b_att])
            P.tt(attm[:, ai, :], att_ps[:, bs], tri[:], ALU.mult, [b_att, b_tri], [b_attm[ai]])
            P.mm(o_ps[:, bs], v_t[:, blk, :], attm[:, ai, :], True, False, [b_v, b_attm[ai]], [b_o])
            for ch in range(2):
                c = blk * 2 + ch
                cs_ = slice(c * 64, (c + 1) * 64)
                P.mm(o_ps[:, cs_], Sb[:, :], QK[:, 2, cs_], False, ch == 1, [b_Sb, b_QK[2]], [b_o])
                dcol = slice((c % 4) * 128, (c % 4 + 1) * 128)
                P.mm(dS_ps[0:64, dcol], K4[ch * 64:(ch + 1) * 64, blk, :], v_t[ch * 64:(ch + 1) * 64, blk, :],
                     True, True, [b_K4, b_v], [b_dS])
                P.stt(St[:], St[:], dec[:, c:c + 1], dS_ps[0:64, dcol], ALU.mult, ALU.add,
                      [b_S, b_dec, b_dS], [b_S])
                P.copy(Sb[:], St[:], [b_S], [b_Sb], eng="act")

    def normout(t):
        par = t % 2
        c0 = t * TT
        r_t, b_r = r_t2[par], b_r2[par]
        P.act(sq[:], o_ps[:, :], AF.Square, [b_o], [b_sq])
        P.copy(oT[:], o_ps[:, :], [b_o, b_sq], [b_oT])
        P.mm(ss_ps[:, :], ones[:, :], sq[:], True, True, [b_ones, b_sq], [b_ss])
        P.act(sv[:], ss_ps[:, :], AF.Ln, [b_ss], [b_sv], bias=float(128 * EPS))
        P.act(rs[:], sv[:], AF.Exp, [b_sv], [b_rs], scale=-0.5)
        P.act(sr[:], r_t[:], AF.Tanh, [b_r], [b_sr], scale=0.5)
        P.stt(sr[:], sr[:], 1.0, r_t[:], ALU.add, ALU.mult, [b_sr, b_r], [b_sr])
        P.stt(oT[:], oT[:], gn[:, 0:1], rs[:], ALU.mult, ALU.mult, [b_oT, b_gn, b_rs], [b_oT])
        P.tt(ot[:], oT[:], sr[:], ALU.mult, [b_oT, b_sr], [b_ot])
        P.dma("sp", ob_d[:, c0:c0 + TT], ot[:], [b_ot], [])

    if NT > 0:
        setup(0)
    for t in range(NT):
        if t + 1 < NT:
            setup(t + 1)
        chunks(t)
        normout(t)


_LBPARTS = {"attn": True, "gla": True}
_NHEAD = [2]


def build_LB_prog():
    P, A, (qa_d, ka_d, v_d), oa_d = build_LB()
    if _LBPARTS["attn"]:
        for hi in range(_NHEAD[0]):
            load_qk(P, A, qa_d[hi], ka_d[hi], 67)
            load_vaug(P, A, v_d[hi])
            attention_head(A, 0, 67, 0.125, blocks_A(0.0), None, oa_d, hi * 64, btab_h=hi)
    if _LBPARTS["gla"]:
        gla_head(P, A)
    return P.finalize()


def build_LD_prog():
    P = Prog()
    A = AttnCtx(P, len(MS_C) + len(MS_D), NS=6, LOOK=4)
    nc = P.nc
    masks_d = P.din("masks", [128, len(MS_C) + len(MS_D), TT], BF16)
    btab_d = P.din("btab", [128, 2, 20], F32)
    sink_d = P.din("sinks", [128, 2], F32)
    sq_d = [P.din(f"sw_qaT{i}", [67, S], BF16) for i in range(2)]
    sk_d = P.din("sw_kaT", [67, S], BF16)
    sv_d = P.din("sw_v", [128, NB, 128], BF16)
    mq_d = [P.din(f"m_qT{i}", [96, S], BF16) for i in range(2)]
    mk_d = [P.din(f"m_kT{i}", [96, S], BF16) for i in range(2)]
    mv_d = [P.din(f"m_v{i}", [128, NB, 128], BF16) for i in range(2)]
    oc_d = P.dout("ocT", [128, S], BF16)
    od_d = P.dout("odT", [128, S], BF16)
    P.dma("sp", A.masks[:], masks_d[:, :, :], [], [A.b_masks])
    P.dma("sp", A.btab[:], btab_d[:, :, :], [], [A.b_btab])
    esink = P.sb("esink", [128, 2], F32)
    b_es = P.buf("esink")
    P.dma("sp", esink[:], sink_d[:, :], [], [b_es])
    P.act(esink[:], esink[:], AF.Exp, [b_es], [b_es])
    load_vaug(P, A, sv_d)
    for hi in range(2):
        load_qk(P, A, sq_d[hi], sk_d if hi == 0 else None, 67)
        attention_head(A, 0, 67, 0.125, blocks_C(0.0), None, oc_d, hi * 64, den_add=(esink, b_es, hi), btab_h=hi)
    for hi in range(2):
        for i in range(4):
            cs_ = slice(i * 4096, (i + 1) * 4096)
            P.dma("sp", A.QT[0:96, cs_], mq_d[hi][:, cs_], [], [A.b_QT])
            P.dma("sp", A.KT[0:96, cs_], mk_d[hi][:, cs_], [], [A.b_KT])
        load_vaug(P, A, mv_d[hi])
        attention_head(A, 0, 96, 96.0 ** -0.5, blocks_D, None, od_d, hi * 64)
    return P.finalize()


_PROGS = {}
_DBG = {}
_RESUME = {}
_STOP = [None]


def _prog(name, fn):
    if name not in _PROGS:
        _PROGS[name] = fn()
    return _PROGS[name]


def _gl(g, kc):
    return np.ascontiguousarray(np.asarray(g, np.float32).reshape(kc, 128).T)


_TRACE = [False]
_NTILES = [None]
_TIMES = []


def _run(nc, ins):
    if _TRACE[0]:
        res = run_bass_kernel_spmd(nc, ins, core_ids=list(range(NCORES)), trace=True)
        _TIMES.append(res.exec_time_ns)
    else:
        res = run_bass_kernel_spmd(nc, ins, core_ids=list(range(NCORES)))
    return res.results


def _vaug(vT):
    v = np.ones((128, NB, 128), NPBF)
    v[:, :, 0:64] = vT.T.reshape(NB, 128, 64).transpose(1, 0, 2)
    return v


def _vblk(vT):
    return np.ascontiguousarray(vT.T.reshape(NB, 128, 128).transpose(1, 0, 2))


def _aug_rows(slope):
    k = np.arange(S)
    iq = k % TT
    kr = np.stack([np.ones(S), np.ones(S), (k % 128).astype(np.float64)]).astype(np.float32)
    qr = np.stack([-8.0 * slope * (iq % 256), -8.0 * slope * (iq - iq % 256), np.full(S, 8.0 * slope)]).astype(np.float32)
    return kr.astype(NPBF), qr.astype(NPBF)


def _alibi(n):
    return [2.0 ** (-8.0 * (i + 1) / n) for i in range(n)]


def _rope_tables(pos):
    r = np.arange(128)
    i = r % 32
    idx = i % 16
    freqs = (np.float32(10000.0) ** (-(idx.astype(np.float32)) / np.float32(16.0))).astype(np.float32)
    ang = pos.astype(np.float32)[None, :] * freqs[:, None]
    cos = np.cos(ang).astype(np.float32)
    sin = np.sin(ang).astype(np.float32)
    sgn = np.where(i < 16, -1.0, 1.0).astype(np.float32)[:, None]
    return np.ascontiguousarray(cos), np.ascontiguousarray(sin * sgn)


def _btab(slopes2):
    t = np.zeros((128, 2, 20), np.float32)
    for i, s_ in enumerate(slopes2):
        for mi in range(20):
            t[:, i, mi] = -s_ * 128.0 * (mi - 3)
    return t


def kernel(x, norm_mix_pre, norm_mix_post, norm_ffn_pre, norm_ffn_post,
           ffn_w_gate, ffn_w_up, ffn_w_down,
           ab_w_in, ab_w_out, gla_w_gate_up, gla_b_gate, gla_norm,
           cd_w_in, cd_w_out, swa_sinks, mla_q_norm, mla_w_uq, mla_kv_norm, mla_w_ukv):
    f32 = lambda a: np.ascontiguousarray(np.asarray(a, np.float32))
    x = f32(x)
    cores = [(c // 4, c % 4) for c in range(NCORES)]
    C = np.ascontiguousarray
    if "P0" in _RESUME:
        P0 = _RESUME["P0"]
    nc = _prog("LA", build_LA) if "P0" not in _RESUME else None
    ins = [{"xT": C(x[b, j * TSH:(j + 1) * TSH, :].T), "w_in": f32(ab_w_in[0]), "gpre": _gl(norm_mix_pre[0], 8)}
           for (b, j) in cores]
    r = _run(nc, ins) if nc is not None else None
    if r is not None:
      P0 = [np.concatenate([r[4 * b + j]["pT"] for j in range(4)], axis=1) for b in range(2)]
    _DBG['P0'] = P0
    nc = _prog("LB", build_LB_prog)
    slA = _alibi(8)
    masksA = make_masks(MS_A, mult_A)
    tri = np.zeros((128, 128), np.float32)
    for s_ in range(128):
        for c_ in range(128):
            if s_ <= c_ and s_ // 64 == c_ // 64:
                tri[s_, c_] = 1.0
    tri = tri.astype(NPBF)
    ident = np.eye(64, dtype=np.float32).astype(NPBF)
    ins = []
    for (b, hp) in cores:
        p = P0[b]
        d = {"masks": masksA, "btab": _btab([slA[2 * hp], slA[2 * hp + 1]])}
        for i in range(2):
            h = 2 * hp + i
            kr, qr = _aug_rows(slA[h])
            d[f"qaT{i}"] = C(np.concatenate([p[h * 64:(h + 1) * 64], qr], axis=0))
            d[f"kaT{i}"] = C(np.concatenate([p[512 + h * 64:512 + (h + 1) * 64], kr], axis=0))
            d[f"v{i}"] = _vaug(p[1024 + h * 64:1024 + (h + 1) * 64])
        gh = hp
        d["gq"] = C(p[1536 + gh * 64:1536 + (gh + 1) * 64])
        d["gk"] = C(p[1792 + gh * 64:1792 + (gh + 1) * 64])
        d["gv"] = _vblk(p[2048 + gh * 128:2048 + (gh + 1) * 128])
        d["gr"] = C(p[2560 + gh * 128:2560 + (gh + 1) * 128])
        d["gg"] = C(p[3072:3088])
        d["gwup"] = f32(gla_w_gate_up[0][:, gh * 64:(gh + 1) * 64])
        d["gnb"] = f32(gla_b_gate[0][gh * 64:(gh + 1) * 64][:, None])
        d["ggn"] = f32(gla_norm[0][:, None])
        d["gtri"] = tri
        d["gident"] = ident
        ins.append(d)
    r = _run(nc, ins)
    om = []
    for b in range(2):
        oa = np.concatenate([r[4 * b + hp]["oaT"] for hp in range(4)], axis=0)
        ob = np.concatenate([r[4 * b + hp]["obT"] for hp in range(4)], axis=0)
        om.append(np.concatenate([oa, ob], axis=0))
    _DBG['om'] = om
    if _STOP[0] == 'LB':
        return None
    nc = _prog("LE", lambda: build_LCE(False))
    ins = []
    for (b, j) in cores:
        ins.append({"xT": C(x[b, j * TSH:(j + 1) * TSH, :].T),
                    "m_omix": C(om[b][:, j * TSH:(j + 1) * TSH]), "m_wo": f32(ab_w_out[0]),
                    "m_gmpost": _gl(norm_mix_post[0], 8),
                    "f_wg": f32(ffn_w_gate[0]), "f_wu": f32(ffn_w_up[0]), "f_wd": f32(ffn_w_down[0]),
                    "f_gpre": _gl(norm_ffn_pre[0], 8), "f_gpost": _gl(norm_ffn_post[0], 8)})
    r = _run(nc, ins)
    x1T = [r[c]["xoT"] for c in range(NCORES)]
    nc = _prog("LC", lambda: build_LCE(True))
    w1 = f32(cd_w_in[0])
    w1x = np.concatenate([w1, w1[:, 1424:1440], w1[:, 1408:1424]], axis=1)
    wuq = f32(mla_w_uq[0]).reshape(384, 8, 96)
    wuq_r = np.concatenate([wuq[:, :, 0:64].reshape(384, 512), wuq[:, :, 64:96].reshape(384, 256),
                            np.concatenate([wuq[:, :, 80:96], wuq[:, :, 64:80]], axis=2).reshape(384, 256)], axis=1)
    ins = []
    for (b, j) in cores:
        cosT, sinT = _rope_tables(np.arange(j * TSH, (j + 1) * TSH))
        ins.append({"xT": C(x1T[4 * b + j]),
                    "w_in1": C(w1x), "gpre1": _gl(norm_mix_pre[1], 8), "w_uq": C(wuq_r),
                    "w_ukv": f32(mla_w_ukv[0]), "gq": _gl(mla_q_norm[0], 3), "gkv": _gl(mla_kv_norm[0], 2),
                    "cosT": cosT, "sinT": sinT})
    r = _run(nc, ins)
    P1 = [np.concatenate([r[4 * b + j]["p1T"] for j in range(4)], axis=1) for b in range(2)]
    _DBG['x1T'] = x1T
    _DBG['P1'] = P1
    nc = _prog("LD", build_LD_prog)
    slC = _alibi(8)
    swa_fn = lambda rel: ((rel >= 0) & (rel <= 127)).astype(np.float32)
    mla_fn = lambda rel: (rel >= 0).astype(np.float32)
    masksD = np.concatenate([make_masks(MS_C, swa_fn), make_masks(MS_D, mla_fn)], axis=1)
    sinks = f32(swa_sinks[0])
    ins = []
    for (b, hp) in cores:
        p = P1[b]
        kvh = hp // 2
        d = {"masks": C(masksD), "btab": _btab([slC[2 * hp], slC[2 * hp + 1]]),
             "sinks": C(np.tile(sinks[2 * hp:2 * hp + 2][None, :], (128, 1))),
             "sw_v": _vaug(p[640 + kvh * 64:640 + (kvh + 1) * 64])}
        for i in range(2):
            h = 2 * hp + i
            kr, qr = _aug_rows(slC[h])
            d[f"sw_qaT{i}"] = C(np.concatenate([p[h * 64:(h + 1) * 64], qr], axis=0))
            if i == 0:
                d["sw_kaT"] = C(np.concatenate([p[512 + kvh * 64:512 + (kvh + 1) * 64], kr], axis=0))
            d[f"m_qT{i}"] = C(np.concatenate([p[768 + h * 64:768 + (h + 1) * 64],
                                              p[1280 + h * 32:1280 + (h + 1) * 32]], axis=0))
            d[f"m_kT{i}"] = C(np.concatenate([p[1536 + h * 128:1536 + h * 128 + 64], p[2560:2592]], axis=0))
            d[f"m_v{i}"] = _vaug(p[1536 + h * 128 + 64:1536 + (h + 1) * 128])
        ins.append(d)
    r = _run(nc, ins)
    om1 = []
    for b in range(2):
        oc = np.concatenate([r[4 * b + hp]["ocT"] for hp in range(4)], axis=0)
        od = np.concatenate([r[4 * b + hp]["odT"] for hp in range(4)], axis=0)
        om1.append(np.concatenate([oc, od], axis=0))
    _DBG['om1'] = om1
    nc = _prog("LE", lambda: build_LCE(False))
    ins = []
    for c, (b, j) in enumerate(cores):
        ins.append({"xT": C(x1T[c]), "m_omix": C(om1[b][:, j * TSH:(j + 1) * TSH]), "m_wo": f32(cd_w_out[0]),
                    "m_gmpost": _gl(norm_mix_post[1], 8),
                    "f_wg": f32(ffn_w_gate[1]), "f_wu": f32(ffn_w_up[1]), "f_wd": f32(ffn_w_down[1]),
                    "f_gpre": _gl(norm_ffn_pre[1], 8), "f_gpost": _gl(norm_ffn_post[1], 8)})
    r = _run(nc, ins)
    out = np.empty((2, S, DM), np.float32)
    for c, (b, j) in enumerate(cores):
        out[b, j * TSH:(j + 1) * TSH, :] = r[c]["xoT"].T
    return out
```
